# Optimizing a Trainium2 kernel written in Bass

```python
import math
import jax
import jax.numpy as jnp
from jax import lax
import numpy as np

D_MODEL = 1024
BATCH = 2
SEQ = 16384
DEPTH = 4

HEAD_DIM = 64
N_MIXERS = 4
GROUP_WIDTH = D_MODEL // N_MIXERS
MIX_WIDTH = N_MIXERS * GROUP_WIDTH
ROPE_THETA = 500000.0
ROPE_FRACTION = 4
Q_BLOCK = 128
NEG_INF = -1e30

DIFF_HEADS = GROUP_WIDTH // HEAD_DIM
DIFF_QK_DIM = HEAD_DIM // 2

DIL_HEADS = GROUP_WIDTH // HEAD_DIM
DIL_PATTERNS = ((128, 1), (512, 4), (2048, 16))

POOL_WINDOWS = (2, 4, 8, 16)
POOL_GROUP = GROUP_WIDTH // len(POOL_WINDOWS)

MLA_HEADS = GROUP_WIDTH // HEAD_DIM
MLA_Q_RANK = GROUP_WIDTH
MLA_KV_RANK = D_MODEL // 8
MLA_NOPE_DIM = HEAD_DIM
MLA_ROPE_DIM = HEAD_DIM // 2
MLA_V_DIM = HEAD_DIM

A_COLS = 3 * GROUP_WIDTH
B_COLS = 3 * GROUP_WIDTH
C_COLS = GROUP_WIDTH
D_COLS = MLA_Q_RANK + MLA_KV_RANK + MLA_ROPE_DIM
IN_COLS = A_COLS + B_COLS + C_COLS + D_COLS

D_FF = 2816
CONV_WIDTH = 3

DN_ALPHA = (2 * DEPTH) ** 0.25
DN_BETA = (8 * DEPTH) ** -0.25
LN_EPS = 1e-5
RMS_EPS = 1e-6

kernel_name = "hybrid_parallel_head_encoder"


def _layer_norm(x, g, b):
    xf = x.astype(jnp.float32)
    mu = jnp.mean(xf, axis=-1, keepdims=True)
    xc = xf - mu
    var = jnp.mean(xc * xc, axis=-1, keepdims=True)
    return (xc * lax.rsqrt(var + LN_EPS) * g + b).astype(x.dtype)


def _rms_norm(x, g):
    xf = x.astype(jnp.float32)
    y = xf * lax.rsqrt(jnp.mean(xf * xf, axis=-1, keepdims=True) + RMS_EPS)
    return (y * g).astype(x.dtype)


def _rope(x, positions, rot_dim):
    half = rot_dim // 2
    inv_freq = ROPE_THETA ** (-jnp.arange(half, dtype=jnp.float32) * 2.0 / rot_dim)
    ang = positions.astype(jnp.float32)[..., None] * inv_freq
    cos = jnp.cos(ang)[:, :, None, :]
    sin = jnp.sin(ang)[:, :, None, :]
    xr = x[..., :rot_dim].astype(jnp.float32)
    x1, x2 = xr[..., :half], xr[..., half:]
    rot = jnp.concatenate([x1 * cos - x2 * sin, x1 * sin + x2 * cos], axis=-1).astype(x.dtype)
    return jnp.concatenate([rot, x[..., rot_dim:]], axis=-1)


def _to_qblocks(t):
    b, s, h, d = t.shape
    return t.reshape(b, s // Q_BLOCK, Q_BLOCK, h, d).swapaxes(0, 1)


def _from_qblocks(t):
    nb, b, qb, h, d = t.shape
    return t.swapaxes(0, 1).reshape(b, nb * qb, h, d)


def _dense_attention(q, k, v, scale):
    def block(qb):
        s = jnp.einsum('bqhd,bkhd->bhqk', qb, k).astype(jnp.float32) * scale
        p = jax.nn.softmax(s, axis=-1)
        return jnp.einsum('bhqk,bkhd->bqhd', p.astype(v.dtype), v)
    return _from_qblocks(lax.map(block, _to_qblocks(q)))


def _diff_attention(q1, k1, q2, k2, v, lam, scale):
    def block(qs):
        qa, qb = qs
        p1 = jax.nn.softmax(jnp.einsum('bqhd,bkhd->bhqk', qa, k1).astype(jnp.float32) * scale, axis=-1)
        p2 = jax.nn.softmax(jnp.einsum('bqhd,bkhd->bhqk', qb, k2).astype(jnp.float32) * scale, axis=-1)
        w = p1 - lam * p2
        return jnp.einsum('bhqk,bkhd->bqhd', w.astype(v.dtype), v)
    return _from_qblocks(lax.map(block, (_to_qblocks(q1), _to_qblocks(q2))))


def _banded_attention(q, k, v, half):
    n, l, h, d = q.shape
    nb = -(-l // Q_BLOCK)
    lp = nb * Q_BLOCK
    span = Q_BLOCK + 2 * half
    qp = jnp.pad(q, ((0, 0), (0, lp - l), (0, 0), (0, 0))).reshape(n, nb, Q_BLOCK, h, d)
    kpad = ((0, 0), (half, half + lp - l), (0, 0), (0, 0))
    kp = jnp.pad(k, kpad)
    vp = jnp.pad(v, kpad)
    idx = jnp.arange(nb)[:, None] * Q_BLOCK + jnp.arange(span)[None, :]
    kb = kp[:, idx]
    vb = vp[:, idx]
    s = jnp.einsum('nbqhd,nbkhd->nbhqk', qp, kb).astype(jnp.float32) * (d ** -0.5)
    key_pos = idx - half
    q_pos = jnp.arange(nb)[:, None] * Q_BLOCK + jnp.arange(Q_BLOCK)[None, :]
    rel = key_pos[:, None, :] - q_pos[:, :, None]
    valid = (jnp.abs(rel) <= half) & (key_pos[:, None, :] >= 0) & (key_pos[:, None, :] < l)
    s = jnp.where(valid[None, :, None, :, :], s, NEG_INF)
    m = jnp.max(s, axis=-1, keepdims=True)
    e = jnp.exp(s - m)
    den = jnp.sum(e, axis=-1, keepdims=True)
    lse = (m + jnp.log(den))[..., 0]
    o = jnp.einsum('nbhqk,nbkhd->nbqhd', (e / den).astype(v.dtype), vb)
    o = o.reshape(n, lp, h, d)[:, :l]
    lse = lse.swapaxes(2, 3).reshape(n, lp, h)[:, :l]
    return o, lse


def _to_strided(t, dil):
    b, s = t.shape[:2]
    rest = t.shape[2:]
    t = t.reshape((b, s // dil, dil) + rest).swapaxes(1, 2)
    return t.reshape((b * dil, s // dil) + rest)


def _from_strided(t, b, dil):
    l = t.shape[1]
    rest = t.shape[2:]
    t = t.reshape((b, dil, l) + rest).swapaxes(1, 2)
    return t.reshape((b, l * dil) + rest)


def _diff_mixer(h_a, positions, lam_params, subln_g, layer_idx):
    b, s, _ = h_a.shape
    q, k, v = jnp.split(h_a, 3, axis=-1)
    q = q.reshape(b, s, DIFF_HEADS, 2, DIFF_QK_DIM)
    k = k.reshape(b, s, DIFF_HEADS, 2, DIFF_QK_DIM)
    v = v.reshape(b, s, DIFF_HEADS, HEAD_DIM)
    rot = DIFF_QK_DIM // ROPE_FRACTION
    q1 = _rope(q[:, :, :, 0], positions, rot)
    q2 = _rope(q[:, :, :, 1], positions, rot)
    k1 = _rope(k[:, :, :, 0], positions, rot)
    k2 = _rope(k[:, :, :, 1], positions, rot)
    lam_init = 0.8 - 0.6 * math.exp(-0.3 * layer_idx)
    lp = lam_params.astype(jnp.float32)
    lam = jnp.exp(jnp.sum(lp[0] * lp[1])) - jnp.exp(jnp.sum(lp[2] * lp[3])) + lam_init
    o = _diff_attention(q1, k1, q2, k2, v, lam, DIFF_QK_DIM ** -0.5)
    o = _rms_norm(o, subln_g) * (1.0 - lam_init)
    return o.reshape(b, s, GROUP_WIDTH)


def _dilated_mixer(h_b, positions):
    b, s, _ = h_b.shape
    q, k, v = jnp.split(h_b, 3, axis=-1)
    rot = HEAD_DIM // ROPE_FRACTION
    q = _rope(q.reshape(b, s, DIL_HEADS, HEAD_DIM), positions, rot)
    k = _rope(k.reshape(b, s, DIL_HEADS, HEAD_DIM), positions, rot)
    v = v.reshape(b, s, DIL_HEADS, HEAD_DIM)
    outs, lses = [], []
    for window, dil in DIL_PATTERNS:
        half = window // 2 // dil
        o, lse = _banded_attention(_to_strided(q, dil), _to_strided(k, dil), _to_strided(v, dil), half)
        outs.append(_from_strided(o, b, dil))
        lses.append(_from_strided(lse, b, dil))
    wts = jax.nn.softmax(jnp.stack(lses, axis=0), axis=0)
    o = jnp.sum(wts[..., None] * jnp.stack(outs, axis=0).astype(jnp.float32), axis=0)
    return o.astype(h_b.dtype).reshape(b, s, GROUP_WIDTH)


def _pool_mixer(h_c, pool_w, pool_scale):
    b, s, _ = h_c.shape
    u = h_c.astype(jnp.float32)
    cs = jnp.concatenate([jnp.zeros((b, 1, GROUP_WIDTH), jnp.float32), jnp.cumsum(u, axis=1)], axis=1)
    t = jnp.arange(s)
    diffs = []
    for g, w in enumerate(POOL_WINDOWS):
        lo = jnp.clip(t - w // 2, 0, s - 1)
        hi = jnp.clip(t + w - w // 2 - 1, 0, s - 1)
        csg = cs[..., g * POOL_GROUP:(g + 1) * POOL_GROUP]
        cnt = (hi - lo + 1).astype(jnp.float32)[None, :, None]
        mean = (csg[:, hi + 1] - csg[:, lo]) / cnt
        diffs.append(mean - u[..., g * POOL_GROUP:(g + 1) * POOL_GROUP])
    d = jnp.stack(diffs, axis=2).astype(h_c.dtype)
    y = jnp.einsum('bsgc,gcd->bsgd', d, pool_w).reshape(b, s, GROUP_WIDTH)
    return y * pool_scale


def _mla_mixer(h_d, positions, q_norm_g, kv_norm_g, w_uq, w_ukv):
    b, s, _ = h_d.shape
    c_q, c_kv, k_r = jnp.split(h_d, [MLA_Q_RANK, MLA_Q_RANK + MLA_KV_RANK], axis=-1)
    q = (_rms_norm(c_q, q_norm_g) @ w_uq).reshape(b, s, MLA_HEADS, MLA_NOPE_DIM + MLA_ROPE_DIM)
    q = jnp.concatenate([q[..., :MLA_NOPE_DIM], _rope(q[..., MLA_NOPE_DIM:], positions, MLA_ROPE_DIM)], axis=-1)
    kv = (_rms_norm(c_kv, kv_norm_g) @ w_ukv).reshape(b, s, MLA_HEADS, MLA_NOPE_DIM + MLA_V_DIM)
    k_nope, v = kv[..., :MLA_NOPE_DIM], kv[..., MLA_NOPE_DIM:]
    k_rope = _rope(k_r[:, :, None, :], positions, MLA_ROPE_DIM)
    k = jnp.concatenate([k_nope, jnp.broadcast_to(k_rope, (b, s, MLA_HEADS, MLA_ROPE_DIM))], axis=-1)
    o = _dense_attention(q, k, v, (MLA_NOPE_DIM + MLA_ROPE_DIM) ** -0.5)
    return o.reshape(b, s, GROUP_WIDTH)


def _conv_ffn(x, w_up, conv_w, conv_b, w_down):
    u = x @ w_up
    u = lax.conv_general_dilated(u, conv_w[:, None, :], window_strides=(1,),
                                 padding=((CONV_WIDTH // 2, CONV_WIDTH // 2),),
                                 dimension_numbers=('NWC', 'WIO', 'NWC'),
                                 feature_group_count=u.shape[-1]) + conv_b
    gate, up = jnp.split(u, 2, axis=-1)
    return (jax.nn.silu(gate) * up) @ w_down


def setup_inputs(seed: int = 0) -> dict:
    key = jax.random.key(seed)
    ks = jax.random.split(key, 20)
    f32 = jnp.float32
    nrm = lambda k, shape, scale: jax.random.normal(k, shape, f32) * scale
    x = jax.random.normal(ks[0], (BATCH, SEQ, D_MODEL), f32)
    offsets = jax.random.randint(ks[1], (BATCH, 1), 0, 4096, dtype=jnp.int32)
    positions = offsets + jnp.arange(SEQ, dtype=jnp.int32)[None, :]
    return {
        'x': x,
        'positions': positions,
        'w_in': nrm(ks[2], (DEPTH, D_MODEL, IN_COLS), D_MODEL ** -0.5),
        'diff_lambda': nrm(ks[3], (DEPTH, 4, DIFF_QK_DIM), 0.1),
        'diff_subln': 1.0 + nrm(ks[4], (DEPTH, HEAD_DIM), 0.02),
        'pool_w': nrm(ks[5], (DEPTH, len(POOL_WINDOWS), POOL_GROUP, POOL_GROUP), POOL_GROUP ** -0.5),
        'pool_scale': 1.0 + nrm(ks[6], (DEPTH, GROUP_WIDTH), 0.1),
        'mla_q_norm': 1.0 + nrm(ks[7], (DEPTH, MLA_Q_RANK), 0.02),
        'mla_kv_norm': 1.0 + nrm(ks[8], (DEPTH, MLA_KV_RANK), 0.02),
        'mla_w_uq': nrm(ks[9], (DEPTH, MLA_Q_RANK, MLA_HEADS * (MLA_NOPE_DIM + MLA_ROPE_DIM)), MLA_Q_RANK ** -0.5),
        'mla_w_ukv': nrm(ks[10], (DEPTH, MLA_KV_RANK, MLA_HEADS * (MLA_NOPE_DIM + MLA_V_DIM)), MLA_KV_RANK ** -0.5),
        'w_out': nrm(ks[11], (DEPTH, MIX_WIDTH, D_MODEL), DN_BETA * MIX_WIDTH ** -0.5),
        'ln1_g': 1.0 + nrm(ks[12], (DEPTH, D_MODEL), 0.02),
        'ln1_b': nrm(ks[13], (DEPTH, D_MODEL), 0.02),
        'ffn_w_up': nrm(ks[14], (DEPTH, D_MODEL, 2 * D_FF), D_MODEL ** -0.5),
        'ffn_conv_w': nrm(ks[15], (DEPTH, CONV_WIDTH, 2 * D_FF), CONV_WIDTH ** -0.5),
        'ffn_conv_b': nrm(ks[16], (DEPTH, 2 * D_FF), 0.02),
        'ffn_w_down': nrm(ks[17], (DEPTH, D_FF, D_MODEL), DN_BETA * D_FF ** -0.5),
        'ln2_g': 1.0 + nrm(ks[18], (DEPTH, D_MODEL), 0.02),
        'ln2_b': nrm(ks[19], (DEPTH, D_MODEL), 0.02),
    }


def reference(x, positions, w_in, diff_lambda, diff_subln, pool_w, pool_scale, mla_q_norm, mla_kv_norm,
              mla_w_uq, mla_w_ukv, w_out, ln1_g, ln1_b, ffn_w_up, ffn_conv_w, ffn_conv_b, ffn_w_down,
              ln2_g, ln2_b):
    for l in range(DEPTH):
        h = x @ w_in[l]
        h_a, h_b, h_c, h_d = jnp.split(h, [A_COLS, A_COLS + B_COLS, A_COLS + B_COLS + C_COLS], axis=-1)
        y_a = _diff_mixer(h_a, positions, diff_lambda[l], diff_subln[l], l)
        y_b = _dilated_mixer(h_b, positions)
        y_c = _pool_mixer(h_c, pool_w[l], pool_scale[l])
        y_d = _mla_mixer(h_d, positions, mla_q_norm[l], mla_kv_norm[l], mla_w_uq[l], mla_w_ukv[l])
        mix = jnp.concatenate([y_a, y_b, y_c, y_d], axis=-1)
        x = _layer_norm(DN_ALPHA * x + mix @ w_out[l], ln1_g[l], ln1_b[l])
        f = _conv_ffn(x, ffn_w_up[l], ffn_conv_w[l], ffn_conv_b[l], ffn_w_down[l])
        x = _layer_norm(DN_ALPHA * x + f, ln2_g[l], ln2_b[l])
    return x
```

```python
import math
import numpy as np
import ml_dtypes
import concourse.bass as bass
import concourse.mybir as mybir
from concourse.bass_utils import run_bass_kernel_spmd

F32 = mybir.dt.float32
BF16 = mybir.dt.bfloat16
I32 = mybir.dt.int32
ALU = mybir.AluOpType
AF = mybir.ActivationFunctionType
NPBF = ml_dtypes.bfloat16

D_MODEL = 1024
SEQ = 16384
DEPTH = 4
NCORE = 8
TOK = 4096
T = 512
NT = TOK // T
D_FF = 2816
ALPHA = (2 * DEPTH) ** 0.25
LN_EPS = 1e-5
RMS_EPS = 1e-6
THETA = 500000.0
TWO_PI = 2.0 * math.pi
CW1 = 6.28125
CW2 = TWO_PI - 6.28125
PI_LO = 3.1415925

R_DQ, R_DK, R_DV, R_BQ, R_BK, R_BV, R_U, R_MQ, R_MKV, R_MKR = 0, 256, 512, 768, 1024, 1280, 1536, 1792, 2176, 2688
NR_OA = 2720
NC_EXT = 26 * 128

COMPUTE = ("pe", "act", "dve", "pool")


class Res:
    __slots__ = ("name", "t", "last_write", "reads")

    def __init__(self, name, t=None):
        self.name = name
        self.t = t
        self.last_write = None
        self.reads = []

    def __getitem__(self, idx):
        return self.t[idx]


class Prog:
    def __init__(self, nc):
        self.nc = nc
        self.ops = {e: [] for e in COMPUTE + ("sp",)}
        self.cnt = {e: 0 for e in COMPUTE}
        self.esem = {e: nc.alloc_semaphore("s_" + e) for e in COMPUTE}
        self.dsems = {}
        self.waited = {e: {} for e in COMPUTE + ("sp",)}
        self.nops = 0

    def sb(self, name, shape, dt):
        return Res(name, self.nc.alloc_sbuf_tensor("sb_" + name, list(shape), dt))

    def ps(self, name, shape, dt=F32):
        return Res(name, self.nc.alloc_psum_tensor("ps_" + name, list(shape), dt))

    def dram(self, name, shape, dt, kind="Internal"):
        return Res(name, self.nc.dram_tensor(name, list(shape), dt, kind=kind))

    def _dsem(self, key):
        if key not in self.dsems:
            self.dsems[key] = [self.nc.alloc_semaphore("d_%d" % len(self.dsems)), 0, 0]
        return self.dsems[key]

    def _collect(self, eng, reads, writes):
        deps = []
        for r in reads:
            if r.last_write is not None:
                deps.append(r.last_write)
        for w in writes:
            if w.last_write is not None:
                deps.append(w.last_write)
            deps.extend(w.reads)
        waits = {}
        for kind, key, val, teng in deps:
            if kind == "c":
                if teng == "pe" and eng == "pe":
                    continue
                waits[("c", key)] = max(waits.get(("c", key), 0), val)
            else:
                st = self.dsems[key]
                st[2] = st[1]
                waits[("d", key)] = max(waits.get(("d", key), 0), st[1])
        out = []
        wd = self.waited[eng]
        for k, v in waits.items():
            if wd.get(k, 0) >= v:
                continue
            wd[k] = v
            out.append((self.esem[k[1]] if k[0] == "c" else self.dsems[k[1]][0], v))
        return out

    def _mark(self, tok, reads, writes):
        for r in reads:
            r.reads.append(tok)
        for w in writes:
            w.last_write = tok
            w.reads = []

    def op(self, eng, fn, reads=(), writes=()):
        reads, writes = list(reads), list(writes)
        waits = self._collect(eng, reads, writes)
        self.cnt[eng] += 1
        tok = ("c", eng, self.cnt[eng], eng)
        self.ops[eng].append((waits, fn, (self.esem[eng], 1)))
        self._mark(tok, reads, writes)
        self.nops += 1 + len(waits)

    def dma(self, q, out_ap, in_ap, reads=(), writes=(), semkey=None):
        reads, writes = list(reads), list(writes)
        if semkey is None:
            semkey = writes[0].name if (writes and writes[0].t is not None and not writes[0].name.startswith("D:")) else (
                reads[0].name if reads else writes[0].name)
        st = self._dsem(semkey)
        waits = self._collect(q, reads, writes)
        if st[2] > 0 and self.waited[q].get(("d", semkey), 0) < st[2]:
            waits.append((st[0], st[2]))
            self.waited[q][("d", semkey)] = st[2]
        st[1] += 16
        tok = ("d", semkey, st[1], q)

        def fn(e, out_ap=out_ap, in_ap=in_ap):
            return e.dma_start(out=out_ap, in_=in_ap)
        self.ops[q].append((waits, fn, (st[0], 16)))
        self._mark(tok, reads, writes)
        self.nops += 1 + len(waits)

    def barrier(self):
        allw = [(("c", e), self.cnt[e]) for e in COMPUTE if self.cnt[e] > 0]
        for key, st in self.dsems.items():
            if st[1] > 0:
                st[2] = st[1]
                allw.append((("d", key), st[1]))
        for eng in COMPUTE + ("sp",):
            waits = []
            wd = self.waited[eng]
            for k, v in allw:
                if wd.get(k, 0) >= v:
                    continue
                wd[k] = v
                waits.append((self.esem[k[1]] if k[0] == "c" else self.dsems[k[1]][0], v))
            self.ops[eng].append((waits, None, None))
            self.nops += len(waits)

    def finish(self, eng, resources):
        waits = self._collect(eng, list(resources), [])
        self.ops[eng].append((waits, None, None))

    def emit(self):
        with self.nc.Block() as block:
            def run(name):
                def body(e):
                    for waits, fn, inc in self.ops[name]:
                        for sem, v in waits:
                            e.wait_ge(sem, v)
                        if fn is not None:
                            fn(e).then_inc(inc[0], inc[1])
                return body
            block.tensor(run("pe"))
            block.scalar(run("act"))
            block.vector(run("dve"))
            block.gpsimd(run("pool"))
            block.sync(run("sp"))


class Arena:
    def __init__(self, tensors):
        self.chunks = [[t, 0, t.shape[1] * 2] for t in tensors]

    def alloc(self, name, shape, dt):
        nbytes = shape[1] * (4 if dt in (F32, I32) else 2)
        nbytes = (nbytes + 63) // 64 * 64
        for ch in self.chunks:
            if ch[1] + nbytes <= ch[2]:
                off = ch[1]
                ch[1] += nbytes
                v = ch[0][0:shape[0], off // 2:(off + nbytes) // 2]
                if dt != BF16:
                    v = v.bitcast(dt)
                return Res(name, v[:, 0:shape[1]])
        raise RuntimeError("arena full: " + name)


class DR:
    def __init__(self, P, name, shape, dt, kind="Internal"):
        self.t = P.nc.dram_tensor(name, list(shape), dt, kind=kind)
        self.name = name
        self.regs = {}
        self.whole = Res("D:" + name)

    def r(self, key=None):
        if key is None:
            return self.whole
        if key not in self.regs:
            self.regs[key] = Res("D:%s:%s" % (self.name, str(key)))
        return self.regs[key]

    def __getitem__(self, idx):
        return self.t[idx]

    def allres(self):
        return [self.whole] + list(self.regs.values())


def bcast_rows(dr, col0, ncols, nparts=128):
    return bass.AP(dr.t, col0, [[0, nparts], [1, ncols]])


def alloc_psum(P):
    return {n: P.ps(n, [128, 512]) for n in ("S0", "S1", "S2", "ACC0", "ACC1", "M0", "M1", "M2")}


def emit_rstd(P, out, src, eps_t, m, ncols):
    P.op("act", lambda e: e.activation(out=out[0:m, 0:ncols], in_=src[0:m, 0:ncols], func=AF.Sqrt, bias=eps_t[0:m, 0:1], scale=1.0),
         [src, eps_t], [out])
    P.op("dve", lambda e: e.reciprocal(out=out[0:m, 0:ncols], in_=out[0:m, 0:ncols]), [out], [out])


def make_eps(P, name, val, alloc=None):
    t = (alloc or P.sb)(name, [128, 1], F32)
    P.op("pool", lambda e: e.memset(t[:, :], val), [], [t])
    return t


def emit_layernorm(P, PS, z, onesLN, vec, gcol, bcol, tmp, outs, ncols=T, store=None):
    mps, vps = PS["M0"], PS["M1"]
    mean_sb, rstd_sb, sq = tmp["mean"], tmp["rstd"], tmp["sq"]
    for o in range(8):
        P.op("pe", lambda e, o=o: e.matmul(mps[:, 0:ncols], lhsT=onesLN[:, :], rhs=z[o][:, 0:ncols],
                                           start=(o == 0), stop=(o == 7)), [onesLN, z[o]], [mps])
    P.op("act", lambda e: e.activation(out=mean_sb[:, 0:ncols], in_=mps[:, 0:ncols], func=AF.Copy), [mps], [mean_sb])
    for o in range(8):
        eng = "dve" if o % 2 == 0 else "pool"
        P.op(eng, lambda e, o=o: e.tensor_tensor(out=z[o][:, 0:ncols], in0=z[o][:, 0:ncols], in1=mean_sb[:, 0:ncols],
                                                 op=ALU.subtract), [z[o], mean_sb], [z[o]])
        s = sq[o % 2]
        P.op("act", lambda e, o=o, s=s: e.activation(out=s[:, 0:ncols], in_=z[o][:, 0:ncols], func=AF.Square), [z[o]], [s])
        P.op("pe", lambda e, o=o, s=s: e.matmul(vps[:, 0:ncols], lhsT=onesLN[:, :], rhs=s[:, 0:ncols],
                                                start=(o == 0), stop=(o == 7)), [onesLN, s], [vps])
    emit_rstd(P, rstd_sb, vps, tmp["eps"], 128, ncols)
    for o in range(8):
        eng = "dve" if o % 2 == 1 else "pool"
        P.op(eng, lambda e, o=o: e.tensor_tensor(out=z[o][:, 0:ncols], in0=z[o][:, 0:ncols], in1=rstd_sb[:, 0:ncols],
                                                 op=ALU.mult), [z[o], rstd_sb], [z[o]])
        P.op("act", lambda e, o=o: e.activation(out=outs[o][:, 0:ncols], in_=z[o][:, 0:ncols], func=AF.Identity,
                                                scale=vec[:, gcol + o:gcol + o + 1], bias=vec[:, bcol + o:bcol + o + 1]),
             [z[o], vec], [outs[o]])
        if store is not None:
            store(o)


def emit_phaseA_setup(P, d, alloc=None):
    W = {}
    if alloc is None:
        alloc = P.sb
    class _A:
        sb = staticmethod(alloc)
    PA = _A
    W["wA"] = d["wbig"]
    wA = W["wA"]
    for k in range(8):
        P.dma("pool", wA.t[:, k * NC_EXT:(k + 1) * NC_EXT], d["w_ext"][k * 128:(k + 1) * 128, :],
              reads=[d["w_ext"].r()], writes=[wA], semkey="wA")
    W["wuq"] = PA.sb("wuq", [128, 2 * 768], BF16)
    for k in range(2):
        P.dma("pool", W["wuq"][:, k * 768:(k + 1) * 768], d["wuq_ext"][k * 128:(k + 1) * 128, :], reads=[d["wuq_ext"].r()],
              writes=[W["wuq"]], semkey="wuq")
    W["wukv"] = PA.sb("wukv", [128, 512], BF16)
    P.dma("pool", W["wukv"][:, :], d["w_ukv"][:, :], reads=[d["w_ukv"].r()], writes=[W["wukv"]], semkey="wukv")
    W["vecA"] = PA.sb("vecA", [128, 16], F32)
    P.dma("sp", W["vecA"][:, :], d["vecA"][:, :], reads=[d["vecA"].r()], writes=[W["vecA"]])
    W["onesq"] = PA.sb("onesq", [128, 128], F32)
    P.op("pool", lambda e: e.memset(W["onesq"][:, :], 1.0 / 256.0), [], [W["onesq"]])
    W["oneskv"] = PA.sb("oneskv", [128, 128], F32)
    P.op("pool", lambda e: e.memset(W["oneskv"][:, :], 1.0 / 128.0), [], [W["oneskv"]])
    W["posi"] = PA.sb("posi", [128, T], I32)
    W["posf"] = PA.sb("posf", [128, T], F32)
    W["ang"] = PA.sb("ang", [128, T], F32)
    W["r1"] = PA.sb("r1", [128, T], F32)
    W["r2"] = PA.sb("r2", [128, T], F32)
    W["kf"] = PA.sb("kf", [128, T], F32)
    W["ki"] = PA.sb("ki", [128, T], I32)
    W["C"] = [PA.sb("ropeC%d" % i, [128, T], F32) for i in range(4)]
    W["S"] = [PA.sb("ropeS%d" % i, [128, T], F32) for i in range(4)]
    W["t1"] = [PA.sb("ra_t1_%d" % i, [128, T], F32) for i in range(2)]
    W["t2"] = [PA.sb("ra_t2_%d" % i, [128, T], F32) for i in range(2)]
    W["ob"] = [PA.sb("ra_ob_%d" % i, [128, T], BF16) for i in range(3)]
    W["cq"] = [PA.sb("cq%d" % i, [128, T], F32) for i in range(3)]
    W["cqsq"] = [PA.sb("cqsq%d" % i, [128, T], F32) for i in range(2)]
    W["rs"] = PA.sb("rs_rstd", [128, T], F32)
    W["cqn"] = [PA.sb("cqn%d" % i, [128, T], BF16) for i in range(3)]
    W["epsr"] = make_eps(P, "epsr_A", RMS_EPS, alloc)
    W["pibias"] = PA.sb("pibias", [128, 1], F32)
    P.op("pool", lambda e: e.memset(W["pibias"][:, :], PI_LO), [], [W["pibias"]])
    return W


def emit_phaseA_tile(P, PS, W, d, t, xb, OA):
    wA, vecA = W["wA"], W["vecA"]
    c0 = t * T
    banks = [PS["S0"], PS["S1"], PS["S2"], PS["ACC0"], PS["ACC1"], PS["M2"]]
    st = {"b": 0, "ob": 0, "t": 0}

    def nb():
        b = banks[st["b"] % len(banks)]
        st["b"] += 1
        return b

    def nob():
        b = W["ob"][st["ob"] % 3]
        st["ob"] += 1
        return b

    def proj(ps, col0, m, rows=128):
        for k in range(8):
            P.op("pe", lambda e, k=k: e.matmul(ps[0:m, :], lhsT=wA.t[0:128, k * NC_EXT + col0:k * NC_EXT + col0 + m],
                                               rhs=xb[k][:, :], start=(k == 0), stop=(k == 7)), [wA, xb[k]], [ps])

    def store(ob, m, row0):
        P.dma("sp", OA[row0:row0 + m, c0:c0 + T], ob[0:m, :], reads=[ob], writes=[OA.r((row0, t))], semkey=ob.name)

    P.dma("sp", W["posi"][:, :], bcast_rows(d["pos"], c0, T), reads=[d["pos"].r()], writes=[W["posi"]])
    P.op("dve", lambda e: e.tensor_copy(out=W["posf"][:, :], in_=W["posi"][:, :]), [W["posi"]], [W["posf"]])
    for i in range(4):
        m = (128, 128, 96, 32)[i]
        ang, r1, r2, kf, ki = W["ang"], W["r1"], W["r2"], W["kf"], W["ki"]
        P.op("dve", lambda e, i=i, m=m: e.tensor_scalar(out=ang[0:m, :], in0=W["posf"][0:m, :],
                                                        scalar1=vecA[0:m, 2 * i:2 * i + 1], scalar2=None, op0=ALU.mult),
             [W["posf"], vecA], [ang])
        P.op("dve", lambda e, m=m: e.tensor_scalar(out=kf[0:m, :], in0=ang[0:m, :], scalar1=1.0 / TWO_PI, scalar2=None, op0=ALU.mult),
             [ang], [kf])
        P.op("dve", lambda e, m=m: e.tensor_copy(out=ki[0:m, :], in_=kf[0:m, :]), [kf], [ki])
        P.op("dve", lambda e, m=m: e.tensor_copy(out=kf[0:m, :], in_=ki[0:m, :]), [ki], [kf])
        P.op("dve", lambda e, m=m: e.scalar_tensor_tensor(out=r1[0:m, :], in0=kf[0:m, :], scalar=-CW1, in1=ang[0:m, :],
                                                          op0=ALU.mult, op1=ALU.add), [kf, ang], [r1])
        P.op("dve", lambda e, m=m: e.scalar_tensor_tensor(out=r1[0:m, :], in0=kf[0:m, :], scalar=-CW2, in1=r1[0:m, :],
                                                          op0=ALU.mult, op1=ALU.add), [kf, r1], [r1])
        P.op("dve", lambda e, m=m: e.tensor_scalar(out=r2[0:m, :], in0=r1[0:m, :], scalar1=0.5 * math.pi, scalar2=None, op0=ALU.add),
             [r1], [r2])
        for rr in (r1, r2):
            for thr, cmp_, per in ((PI_LO, ALU.is_gt, -TWO_PI), (-PI_LO, ALU.is_lt, TWO_PI)):
                P.op("dve", lambda e, m=m, rr=rr, thr=thr, cmp_=cmp_, per=per: e.tensor_scalar(
                    out=ang[0:m, :], in0=rr[0:m, :], scalar1=thr, scalar2=per, op0=cmp_, op1=ALU.mult), [rr], [ang])
                P.op("dve", lambda e, m=m, rr=rr: e.tensor_tensor(out=rr[0:m, :], in0=rr[0:m, :], in1=ang[0:m, :], op=ALU.add),
                     [rr, ang], [rr])
            P.op("dve", lambda e, m=m, rr=rr: e.tensor_scalar(out=rr[0:m, :], in0=rr[0:m, :], scalar1=-PI_LO, scalar2=PI_LO,
                                                              op0=ALU.max, op1=ALU.min), [rr], [rr])
        P.op("act", lambda e, i=i, m=m: e.activation(out=W["S"][i][0:m, :], in_=r1[0:m, :], func=AF.Sin), [r1], [W["S"][i]])
        P.op("act", lambda e, i=i, m=m: e.activation(out=W["C"][i][0:m, :], in_=r2[0:m, :], func=AF.Sin), [r2], [W["C"][i]])
        P.op("dve", lambda e, i=i, m=m: e.tensor_scalar(out=W["S"][i][0:m, :], in0=W["S"][i][0:m, :],
                                                        scalar1=vecA[0:m, 2 * i + 1:2 * i + 2], scalar2=None, op0=ALU.mult),
             [W["S"][i], vecA], [W["S"][i]])

    def rope_out(pa, pb, m, pat, row0):
        i = st["t"] % 2
        st["t"] += 1
        t1, t2 = W["t1"][i], W["t2"][i]
        ob = nob()
        P.op("dve", lambda e: e.tensor_tensor(out=t1[0:m, :], in0=pa[0:m, :], in1=W["C"][pat][0:m, :], op=ALU.mult),
             [pa, W["C"][pat]], [t1])
        P.op("dve", lambda e: e.tensor_tensor(out=t2[0:m, :], in0=pb[0:m, :], in1=W["S"][pat][0:m, :], op=ALU.mult),
             [pb, W["S"][pat]], [t2])
        P.op("pool", lambda e: e.tensor_tensor(out=ob[0:m, :], in0=t1[0:m, :], in1=t2[0:m, :], op=ALU.add), [t1, t2], [ob])
        store(ob, m, row0)

    def plain_out(pa, m, row0):
        ob = nob()
        P.op("act", lambda e: e.activation(out=ob[0:m, :], in_=pa[0:m, :], func=AF.Copy), [pa], [ob])
        store(ob, m, row0)

    for j, row0 in enumerate((R_DQ, R_DQ + 128, R_DK, R_DK + 128)):
        pa, pb = nb(), nb()
        proj(pa, j * 128, 128)
        proj(pb, (4 + j) * 128, 128)
        rope_out(pa, pb, 128, 0, row0)
    for j, row0 in enumerate((R_DV, R_DV + 128)):
        pa = nb()
        proj(pa, (8 + j) * 128, 128)
        plain_out(pa, 128, row0)
    for j, row0 in enumerate((R_BQ, R_BQ + 128, R_BK, R_BK + 128)):
        pa, pb = nb(), nb()
        proj(pa, (10 + j) * 128, 128)
        proj(pb, (14 + j) * 128, 128)
        rope_out(pa, pb, 128, 1, row0)
    for j, row0 in enumerate((R_BV, R_BV + 128, R_U, R_U + 128)):
        pa = nb()
        proj(pa, (18 + j) * 128, 128)
        plain_out(pa, 128, row0)
    pa, pb = nb(), nb()
    proj(pa, 25 * 128, 32)
    proj(pb, 25 * 128 + 32, 32)
    rope_out(pa, pb, 32, 3, R_MKR)
    for j in range(3):
        pa = nb()
        proj(pa, (22 + j) * 128, 128)
        P.op("act", lambda e, j=j, pa=pa: e.activation(out=W["cq"][j][:, :], in_=pa[:, :], func=AF.Copy), [pa], [W["cq"][j]])
    for grp, (chunks, ones, eps_col) in enumerate((((0, 1), W["onesq"], 8), ((2,), W["oneskv"], 10))):
        msp = nb()
        for n, j in enumerate(chunks):
            s = W["cqsq"][n % 2]
            P.op("act", lambda e, j=j, s=s: e.activation(out=s[:, :], in_=W["cq"][j][:, :], func=AF.Square), [W["cq"][j]], [s])
            P.op("pe", lambda e, s=s, n=n, ones=ones, L=len(chunks), msp=msp: e.matmul(msp[:, :], lhsT=ones[:, :], rhs=s[:, :],
                                                                             start=(n == 0), stop=(n == L - 1)),
                 [ones, s], [msp])
        emit_rstd(P, W["rs"], msp, W["epsr"], 128, T)
        for j in chunks:
            P.op("dve", lambda e, j=j: e.tensor_tensor(out=W["cq"][j][:, :], in0=W["cq"][j][:, :], in1=W["rs"][:, :],
                                                       op=ALU.mult), [W["cq"][j], W["rs"]], [W["cq"][j]])
            P.op("dve", lambda e, j=j: e.tensor_scalar(out=W["cqn"][j][:, :], in0=W["cq"][j][:, :],
                                                       scalar1=vecA[:, 8 + j:9 + j], scalar2=None, op0=ALU.mult),
                 [W["cq"][j], vecA], [W["cqn"][j]])
    for h in range(4):
        pa, pb = nb(), nb()
        for which, ps in ((0, pa), (1, pb)):
            col0 = h * 192 + which * 96
            for k in range(2):
                P.op("pe", lambda e, k=k, ps=ps, col0=col0: e.matmul(ps[0:96, :], lhsT=W["wuq"][:, k * 768 + col0:k * 768 + col0 + 96],
                                                                     rhs=W["cqn"][k][:, :], start=(k == 0), stop=(k == 1)),
                     [W["wuq"], W["cqn"][k]], [ps])
        rope_out(pa, pb, 96, 2, R_MQ + 96 * h)
    for h in range(4):
        pa = nb()
        P.op("pe", lambda e, h=h, pa=pa: e.matmul(pa[:, :], lhsT=W["wukv"][:, h * 128:(h + 1) * 128], rhs=W["cqn"][2][:, :],
                                                  start=True, stop=True), [W["wukv"], W["cqn"][2]], [pa])
        plain_out(pa, 128, R_MKV + 128 * h)


def declare_phaseA_inputs(P):
    d = {}
    d["w_ext"] = DR(P, "w_ext", [1024, NC_EXT], F32, "ExternalInput")
    d["wuq_ext"] = DR(P, "wuq_ext", [256, 768], F32, "ExternalInput")
    d["w_ukv"] = DR(P, "w_ukv", [128, 512], F32, "ExternalInput")
    d["vecA"] = DR(P, "vecA", [128, 16], F32, "ExternalInput")
    d["pos"] = DR(P, "pos", [1, TOK], I32, "ExternalInput")
    return d


def build_A():
    nc = bass.Bass("TRN2", target_bir_lowering=False)
    P = Prog(nc)
    d = declare_phaseA_inputs(P)
    xT = DR(P, "xT", [1024, TOK], F32, "ExternalInput")
    OA = DR(P, "OA", [NR_OA, TOK], BF16, "ExternalOutput")
    PS = alloc_psum(P)
    d["wbig"] = P.sb("wbig", [128, 8 * NC_EXT], BF16)
    W = emit_phaseA_setup(P, d)
    xf = [P.sb("xf%d" % i, [128, T], F32) for i in range(2)]
    xb = [[P.sb("xb%d_%d" % (s, k), [128, T], BF16) for k in range(8)] for s in range(2)]
    for t in range(NT):
        for k in range(8):
            f = xf[k % 2]
            P.dma("sp", f[:, :], xT[k * 128:(k + 1) * 128, t * T:(t + 1) * T], reads=[xT.r()], writes=[f])
            P.op("dve" if k % 2 == 0 else "pool", lambda e, f=f, k=k, t=t: e.tensor_copy(out=xb[t % 2][k][:, :], in_=f[:, :]),
                 [f], [xb[t % 2][k]])
        emit_phaseA_tile(P, PS, W, d, t, xb[t % 2], OA)
    P.finish("sp", OA.allres())
    P.emit()
    return nc


def build_BC1():
    nc = bass.Bass("TRN2", target_bir_lowering=False)
    P = Prog(nc)
    QT = DR(P, "QT", [896, TOK], BF16, "ExternalInput")
    dKT = DR(P, "dKT", [256, SEQ], BF16, "ExternalInput")
    dVx = DR(P, "dVx", [4, 128, SEQ], BF16, "ExternalInput")
    bKT = DR(P, "bKT", [256, 6144], BF16, "ExternalInput")
    bVx = DR(P, "bVx", [4, 128, 6144], BF16, "ExternalInput")
    mKT = DR(P, "mKT", [4, 96, SEQ], BF16, "ExternalInput")
    mVx = DR(P, "mVx", [4, 128, SEQ], BF16, "ExternalInput")
    uTok = DR(P, "uTok", [128, 34 * 256], BF16, "ExternalInput")
    u16T = DR(P, "u16T", [256, TOK], BF16, "ExternalInput")
    invc = DR(P, "invc", [4, 64, TOK], F32, "ExternalInput")
    dstrip = DR(P, "dstrip", [128, 2944], BF16, "ExternalInput")
    pstrip = DR(P, "pstrip", [128, 4 * 1152], BF16, "ExternalInput")
    xT = DR(P, "xT", [1024, TOK], F32, "ExternalInput")
    w_out = DR(P, "w_out", [1024, 1024], F32, "ExternalInput")
    pool_w = DR(P, "pool_w", [64, 256], F32, "ExternalInput")
    vecB = DR(P, "vecB", [128, 160], F32, "ExternalInput")
    x1T = DR(P, "x1T", [1024, TOK], F32, "ExternalOutput")
    mixT = DR(P, "mixT", [1024, TOK], BF16)
    PS = alloc_psum(P)

    vec = P.sb("vecB", [128, 160], F32)
    P.dma("sp", vec[:, :], vecB[:, :], reads=[vecB.r()], writes=[vec])
    wo = P.sb("wo", [128, 8 * 1024], BF16)
    for k in range(8):
        P.dma("pool", wo[:, k * 1024:(k + 1) * 1024], w_out[k * 128:(k + 1) * 128, :], reads=[w_out.r()], writes=[wo], semkey="wo")
    pw = P.sb("pw", [64, 256], BF16)
    P.dma("pool", pw[:, :], pool_w[:, :], reads=[pool_w.r()], writes=[pw])
    ds = P.sb("dstrip", [128, 2944], BF16)
    P.dma("sp", ds[:, :], dstrip[:, :], reads=[dstrip.r()], writes=[ds])
    pst = P.sb("pstrip", [128, 4 * 1152], BF16)
    P.dma("sp", pst[:, :], pstrip[:, :], reads=[pstrip.r()], writes=[pst])
    ut = P.sb("utok", [128, 34 * 256], BF16)
    P.dma("sp", ut[:, :], uTok[:, :], reads=[uTok.r()], writes=[ut])
    ones64 = P.sb("ones64", [64, 64], F32)
    P.op("pool", lambda e: e.memset(ones64[:, :], 1.0 / 64.0), [], [ones64])
    onesLN = P.sb("onesLN", [128, 128], F32)
    P.op("pool", lambda e: e.memset(onesLN[:, :], 1.0 / 1024.0), [], [onesLN])
    lt = P.sb("lam_t", [128, 64], F32)
    ls = P.sb("lam_s", [128, 4], F32)
    P.op("dve", lambda e: e.tensor_tensor(out=lt[:, 0:32], in0=vec[:, 32:64], in1=vec[:, 64:96], op=ALU.mult), [vec], [lt])
    P.op("dve", lambda e: e.tensor_tensor(out=lt[:, 32:64], in0=vec[:, 96:128], in1=vec[:, 128:160], op=ALU.mult), [vec, lt], [lt])
    P.op("dve", lambda e: e.reduce_sum(out=ls[:, 0:1], in_=lt[:, 0:32], axis=mybir.AxisListType.X), [lt], [ls])
    P.op("dve", lambda e: e.reduce_sum(out=ls[:, 1:2], in_=lt[:, 32:64], axis=mybir.AxisListType.X), [lt, ls], [ls])
    P.op("act", lambda e: e.activation(out=ls[:, 0:2], in_=ls[:, 0:2], func=AF.Exp), [ls], [ls])
    P.op("dve", lambda e: e.tensor_tensor(out=ls[:, 2:3], in0=ls[:, 1:2], in1=ls[:, 0:1], op=ALU.subtract), [ls], [ls])
    P.op("dve", lambda e: e.tensor_tensor(out=ls[:, 2:3], in0=ls[:, 2:3], in1=vec[:, 21:22], op=ALU.subtract), [ls, vec], [ls])
    P.op("dve", lambda e: e.tensor_tensor(out=ls[:, 3:4], in0=vec[:, 20:21], in1=vec[:, 22:23], op=ALU.mult), [ls, vec], [ls])
    lam = ls

    epsr = make_eps(P, "epsr", RMS_EPS)
    kt = P.sb("kt", [128, SEQ], BF16)
    vx = P.sb("vx", [128, SEQ], BF16)
    qb = [P.sb("q%d" % i, [128, T], BF16) for i in range(2)]
    pT = [P.sb("pT%d" % i, [128, T], BF16) for i in range(4)]
    pT2 = [P.sb("pTm%d" % i, [128, T], BF16) for i in range(2)]
    rden = [P.sb("rden%d" % i, [64, T], F32) for i in range(2)]
    o32 = [P.sb("o32_%d" % i, [64, T], F32) for i in range(2)]
    osq = P.sb("osq", [64, T], F32)
    orst = P.sb("orst", [64, T], F32)
    ob16 = [P.sb("ob16_%d" % i, [64, T], BF16) for i in range(2)]
    Sb = [PS["S0"], PS["S1"], PS["S2"]]
    cnt = {"s": 0, "p": 0, "pm": 0, "q": 0, "ob": 0}

    def attn(d, pbase, nblk, blk0, scale, acc, q, strip_delta0=None):
        for i in range(nblk):
            kb = blk0 + i
            sps = Sb[cnt["s"] % 3]
            cnt["s"] += 1
            p = pT[cnt["p"] % 4]
            cnt["p"] += 1
            P.op("pe", lambda e, kb=kb, sps=sps: e.matmul(sps[:, :], lhsT=kt[pbase:pbase + d, kb * 128:(kb + 1) * 128],
                                                          rhs=q[pbase:pbase + d, :], start=True, stop=True), [kt, q], [sps])
            P.op("act", lambda e, sps=sps, p=p: e.activation(out=p[:, :], in_=sps[:, :], func=AF.Exp, scale=scale), [sps], [p])
            if strip_delta0 is not None:
                c0 = 1408 - 128 * (strip_delta0 + i)
                p2 = pT2[cnt["pm"] % 2]
                cnt["pm"] += 1
                P.op("dve", lambda e, p=p, p2=p2, c0=c0: e.tensor_tensor(out=p2[:, :], in0=p[:, :], in1=ds[:, c0:c0 + T], op=ALU.mult),
                     [p, ds], [p2])
                p = p2
            P.op("pe", lambda e, kb=kb, p=p, i=i: e.matmul(acc[:, :], lhsT=vx[:, kb * 128:(kb + 1) * 128], rhs=p[:, :],
                                                           start=(i == 0), stop=(i == nblk - 1)), [vx, p], [acc])

    def normalize(acc, out32, rd):
        P.op("dve", lambda e: e.reciprocal(out=rd[0:64, :], in_=acc[64:128, :]), [acc], [rd])
        P.op("dve", lambda e: e.tensor_tensor(out=out32[0:64, :], in0=acc[0:64, :], in1=rd[0:64, :], op=ALU.mult), [acc, rd], [out32])

    def store_mix(ob, row0, t):
        P.dma("sp", mixT[row0:row0 + 64, t * T:(t + 1) * T], ob[0:64, :], reads=[ob], writes=[mixT.r((row0 // 128, t))], semkey=ob.name)

    def load_q(row0, d, pbase, t):
        q = qb[cnt["q"] % 2]
        cnt["q"] += 1
        P.dma("sp", q[pbase:pbase + d, :], QT[row0:row0 + d, t * T:(t + 1) * T], reads=[QT.r()], writes=[q])
        return q

    for h in range(4):
        P.dma("sp", kt[0:64, :], dKT[64 * h:64 * h + 64, :], reads=[dKT.r()], writes=[kt])
        P.dma("sp", vx[:, :], dVx[h, :, :], reads=[dVx.r()], writes=[vx])
        for t in range(NT):
            q = load_q(R_DQ + 64 * h, 64, 0, t)
            attn(32, 0, 128, 0, 32 ** -0.5, PS["ACC0"], q)
            attn(32, 32, 128, 0, 32 ** -0.5, PS["ACC1"], q)
            normalize(PS["ACC0"], o32[0], rden[0])
            normalize(PS["ACC1"], o32[1], rden[1])
            o = o32[0]
            P.op("dve", lambda e, o=o: e.scalar_tensor_tensor(out=o[0:64, :], in0=o32[1][0:64, :], scalar=lam[0:64, 2:3], in1=o[0:64, :],
                                                              op0=ALU.mult, op1=ALU.add), [o32[1], o, lam], [o])
            P.op("act", lambda e, o=o: e.activation(out=osq[0:64, :], in_=o[0:64, :], func=AF.Square), [o], [osq])
            P.op("pe", lambda e: e.matmul(PS["M0"][0:64, :], lhsT=ones64[:, :], rhs=osq[0:64, :], start=True, stop=True),
                 [ones64, osq], [PS["M0"]])
            emit_rstd(P, orst, PS["M0"], epsr, 64, T)
            P.op("dve", lambda e, o=o: e.tensor_tensor(out=o[0:64, :], in0=o[0:64, :], in1=orst[0:64, :], op=ALU.mult), [o, orst], [o])
            ob = ob16[cnt["ob"] % 2]
            cnt["ob"] += 1
            P.op("dve", lambda e, o=o, ob=ob: e.tensor_scalar(out=ob[0:64, :], in0=o[0:64, :], scalar1=lam[0:64, 3:4], scalar2=None,
                                                              op0=ALU.mult), [o, lam], [ob])
            store_mix(ob, 64 * h, t)
    for h in range(4):
        P.dma("sp", kt[0:96, :], mKT[h, :, :], reads=[mKT.r()], writes=[kt])
        P.dma("sp", vx[:, :], mVx[h, :, :], reads=[mVx.r()], writes=[vx])
        for t in range(NT):
            q = load_q(512 + 96 * h, 96, 0, t)
            acc = PS["ACC0"] if t % 2 == 0 else PS["ACC1"]
            attn(96, 0, 128, 0, 96 ** -0.5, acc, q)
            i = t % 2
            normalize(acc, o32[i], rden[i])
            ob = ob16[cnt["ob"] % 2]
            cnt["ob"] += 1
            P.op("act", lambda e, i=i, ob=ob: e.activation(out=ob[0:64, :], in_=o32[i][0:64, :], func=AF.Copy), [o32[i]], [ob])
            store_mix(ob, 768 + 64 * h, t)
    for h in range(4):
        P.dma("sp", kt[0:64, 0:6144], bKT[64 * h:64 * h + 64, :], reads=[bKT.r()], writes=[kt])
        P.dma("sp", vx[:, 0:6144], bVx[h, :, :], reads=[bVx.r()], writes=[vx])
        for t in range(NT):
            q = load_q(256 + 64 * h, 64, 0, t)
            acc = PS["ACC0"] if t % 2 == 0 else PS["ACC1"]
            attn(64, 0, 20, 4 * t, 64 ** -0.5, acc, q, strip_delta0=-8)
            i = t % 2
            normalize(acc, o32[i], rden[i])
            ob = ob16[cnt["ob"] % 2]
            cnt["ob"] += 1
            P.op("act", lambda e, i=i, ob=ob: e.activation(out=ob[0:64, :], in_=o32[i][0:64, :], func=AF.Copy), [o32[i]], [ob])
            store_mix(ob, 256 + 64 * h, t)
    ic = [P.sb("ic%d" % i, [64, T], F32) for i in range(2)]
    uu = [P.sb("uu%d" % i, [128, T], BF16) for i in range(2)]
    dd = [P.sb("dd%d" % i, [64, T], F32) for i in range(2)]
    db = [P.sb("db%d" % i, [64, T], BF16) for i in range(2)]
    n = 0
    for t in range(NT):
        for g in range(4):
            i = n % 2
            n += 1
            pp, py = PS["M1"], PS["M2"]
            for dl in range(-1, 5):
                blk = 4 * t + dl + 1
                c0 = g * 1152 + 512 - 128 * dl
                P.op("pe", lambda e, blk=blk, c0=c0, dl=dl, g=g: e.matmul(
                    pp[0:64, :], lhsT=ut[:, blk * 256 + 64 * g:blk * 256 + 64 * g + 64], rhs=pst[:, c0:c0 + T],
                    start=(dl == -1), stop=(dl == 4)), [ut, pst], [pp])
            P.dma("sp", ic[i][:, :], invc[g, :, t * T:(t + 1) * T], reads=[invc.r()], writes=[ic[i]])
            ub = uu[i]
            P.dma("sp", ub[0:64, :], u16T[64 * g:64 * g + 64, t * T:(t + 1) * T], reads=[u16T.r()], writes=[ub])
            pb0 = 0
            P.op("dve", lambda e, i=i: e.tensor_tensor(out=dd[i][0:64, :], in0=pp[0:64, :], in1=ic[i][0:64, :], op=ALU.mult),
                 [pp, ic[i]], [dd[i]])
            P.op("dve", lambda e, i=i, ub=ub, pb0=pb0: e.tensor_tensor(out=db[i][0:64, :], in0=dd[i][0:64, :], in1=ub[pb0:pb0 + 64, :],
                                                                       op=ALU.subtract), [dd[i], ub], [db[i]])
            P.op("pe", lambda e, i=i, g=g: e.matmul(py[0:64, :], lhsT=pw[0:64, 64 * g:64 * g + 64], rhs=db[i][0:64, :],
                                                    start=True, stop=True), [pw, db[i]], [py])
            ob = ob16[cnt["ob"] % 2]
            cnt["ob"] += 1
            P.op("dve", lambda e, ob=ob, g=g: e.tensor_scalar(out=ob[0:64, :], in0=py[0:64, :], scalar1=vec[0:64, 16 + g:17 + g],
                                                              scalar2=None, op0=ALU.mult), [py, vec], [ob])
            store_mix(ob, 512 + 64 * g, t)

    mx = [P.sb("mx%d" % k, [128, T], BF16) for k in range(8)]
    z = [P.sb("z%d" % k, [128, T], F32) for k in range(8)]
    xc = [P.sb("xc%d" % k, [128, T], F32) for k in range(2)]
    oc = [P.sb("oc%d" % k, [128, T], F32) for k in range(2)]
    tmp = {"mean": P.sb("ln_mean", [128, T], F32), "rstd": P.sb("ln_rstd", [128, T], F32),
           "sq": [P.sb("ln_sq%d" % k, [128, T], F32) for k in range(2)], "eps": make_eps(P, "epsln", LN_EPS)}
    for t in range(NT):
        for k in range(8):
            P.dma("sp", mx[k][:, :], mixT[k * 128:(k + 1) * 128, t * T:(t + 1) * T], reads=[mixT.r((k, t))], writes=[mx[k]])
        for o in range(8):
            yps = Sb[o % 3]
            for k in range(8):
                P.op("pe", lambda e, k=k, o=o, yps=yps: e.matmul(yps[:, :], lhsT=wo[:, k * 1024 + o * 128:k * 1024 + (o + 1) * 128],
                                                                 rhs=mx[k][:, :], start=(k == 0), stop=(k == 7)), [wo, mx[k]], [yps])
            x_ = xc[o % 2]
            P.dma("sp", x_[:, :], xT[o * 128:(o + 1) * 128, t * T:(t + 1) * T], reads=[xT.r()], writes=[x_])
            P.op("dve", lambda e, o=o, x_=x_, yps=yps: e.scalar_tensor_tensor(out=z[o][:, :], in0=x_[:, :], scalar=ALPHA, in1=yps[:, :],
                                                                              op0=ALU.mult, op1=ALU.add), [x_, yps], [z[o]])
        outs = [oc[o % 2] for o in range(8)]

        def store(o, t=t):
            P.dma("sp", x1T[o * 128:(o + 1) * 128, t * T:(t + 1) * T], oc[o % 2][:, :], reads=[oc[o % 2]], writes=[x1T.r((o, t))],
                  semkey=oc[o % 2].name)
        emit_layernorm(P, PS, z, onesLN, vec, 0, 8, tmp, outs, store=store)
    P.finish("sp", x1T.allres())
    P.emit()
    return nc


TF = 256
def build_C2A():
    nc = bass.Bass("TRN2", target_bir_lowering=False)
    P = Prog(nc)
    d = declare_phaseA_inputs(P)
    x1h = DR(P, "x1h", [1024, TOK + 2], F32, "ExternalInput")
    w_up = DR(P, "w_up", [1024, 2 * D_FF], F32, "ExternalInput")
    w_dn = DR(P, "w_dn", [D_FF, 1024], F32, "ExternalInput")
    vecC = DR(P, "vecC", [128, 192], F32, "ExternalInput")
    x2T = DR(P, "x2T", [1024, TOK], F32, "ExternalOutput")
    OA = DR(P, "OA", [NR_OA, TOK], BF16, "ExternalOutput")
    PS = alloc_psum(P)
    wbig = P.sb("wbig", [128, 8 * 2 * D_FF], BF16)
    d["wbig"] = wbig
    for k in range(8):
        P.dma("pool", wbig[:, k * 5632:(k + 1) * 5632], w_up[k * 128:(k + 1) * 128, :], reads=[w_up.r()], writes=[wbig], semkey="wbig")
    wdn = P.sb("wdn", [128, 22 * 1024], BF16)
    for j in range(22):
        P.dma("pool", wdn[:, j * 1024:(j + 1) * 1024], w_dn[j * 128:(j + 1) * 128, :], reads=[w_dn.r()], writes=[wdn], semkey="wdn")
    vec = P.sb("vecC", [128, 192], F32)
    P.dma("sp", vec[:, :], vecC[:, :], reads=[vecC.r()], writes=[vec])
    onesLN = P.sb("onesLN", [128, 128], F32)
    P.op("pool", lambda e: e.memset(onesLN[:, :], 1.0 / 1024.0), [], [onesLN])

    x1c = [P.sb("x1c%d" % k, [128, TF + 2], F32) for k in range(8)]
    x1b = [P.sb("x1b%d" % k, [128, TF + 2], BF16) for k in range(8)]
    gb = [P.sb("gb%d" % j, [128, TF], BF16) for j in range(22)]
    acc = [P.sb("acc%d" % i, [128, TF], F32) for i in range(4)]
    sg = [P.sb("sg%d" % i, [128, TF], F32) for i in range(2)]
    z = [P.sb("z%d" % k, [128, TF], F32) for k in range(8)]
    oc = [P.sb("oc%d" % k, [128, TF], F32) for k in range(2)]
    tmp = {"mean": P.sb("ln_mean", [128, TF], F32), "rstd": P.sb("ln_rstd", [128, TF], F32),
           "sq": [P.sb("ln_sq%d" % k, [128, TF], F32) for k in range(2)], "eps": make_eps(P, "epsln", LN_EPS)}
    banks = [PS["S0"], PS["S1"], PS["S2"], PS["ACC0"], PS["ACC1"], PS["M2"]]
    nbk = 0
    na = 0
    for t in range(TOK // TF):
        c0 = t * TF
        for k in range(8):
            P.dma("sp", x1c[k][:, :], x1h[k * 128:(k + 1) * 128, c0:c0 + TF + 2], reads=[x1h.r()], writes=[x1c[k]])
            P.op("pool" if k % 2 == 0 else "dve", lambda e, k=k: e.tensor_copy(out=x1b[k][:, :], in_=x1c[k][:, :]), [x1c[k]], [x1b[k]])
        for j in range(22):
            accs = []
            for which in range(2):
                ch = which * 22 + j
                col0 = ch * 128
                ps = banks[nbk % 6]
                nbk += 1
                for k in range(8):
                    P.op("pe", lambda e, k=k, ps=ps, col0=col0: e.matmul(ps[:, 0:TF], lhsT=wbig[:, k * 5632 + col0:k * 5632 + col0 + 128],
                                                                         rhs=x1b[k][:, 1:TF + 1], start=(k == 0), stop=(k == 7)),
                         [wbig, x1b[k]], [ps])
                for k in range(8):
                    P.op("pe", lambda e, k=k, ps=ps, col0=col0: e.matmul(ps[:, 384:385], lhsT=wbig[:, k * 5632 + col0:k * 5632 + col0 + 128],
                                                                         rhs=x1b[k][:, 0:1], start=(k == 0), stop=(k == 7)),
                         [wbig, x1b[k]], [ps])
                for k in range(8):
                    P.op("pe", lambda e, k=k, ps=ps, col0=col0: e.matmul(ps[:, 448:449], lhsT=wbig[:, k * 5632 + col0:k * 5632 + col0 + 128],
                                                                         rhs=x1b[k][:, TF + 1:TF + 2], start=(k == 0), stop=(k == 7)),
                         [wbig, x1b[k]], [ps])
                a = acc[na % 4]
                na += 1
                w0, w1, w2, bb = (vec[:, 16 + ch:17 + ch], vec[:, 60 + ch:61 + ch], vec[:, 104 + ch:105 + ch], vec[:, 148 + ch:149 + ch])
                P.op("act", lambda e, a=a, ps=ps, w1=w1, bb=bb: e.activation(out=a[:, :], in_=ps[:, 0:TF], func=AF.Identity, scale=w1, bias=bb),
                     [ps, vec], [a])
                P.op("dve", lambda e, a=a, ps=ps, w0=w0: e.scalar_tensor_tensor(out=a[:, 1:TF], in0=ps[:, 0:TF - 1], scalar=w0, in1=a[:, 1:TF],
                                                                                op0=ALU.mult, op1=ALU.add), [ps, a, vec], [a])
                P.op("dve", lambda e, a=a, ps=ps, w2=w2: e.scalar_tensor_tensor(out=a[:, 0:TF - 1], in0=ps[:, 1:TF], scalar=w2, in1=a[:, 0:TF - 1],
                                                                                op0=ALU.mult, op1=ALU.add), [ps, a, vec], [a])
                P.op("dve", lambda e, a=a, ps=ps, w0=w0: e.scalar_tensor_tensor(out=a[:, 0:1], in0=ps[:, 384:385], scalar=w0, in1=a[:, 0:1],
                                                                                op0=ALU.mult, op1=ALU.add), [ps, a, vec], [a])
                P.op("dve", lambda e, a=a, ps=ps, w2=w2: e.scalar_tensor_tensor(out=a[:, TF - 1:TF], in0=ps[:, 448:449], scalar=w2,
                                                                                in1=a[:, TF - 1:TF], op0=ALU.mult, op1=ALU.add),
                     [ps, a, vec], [a])
                accs.append(a)
            s = sg[j % 2]
            P.op("act", lambda e, s=s, a=accs[0]: e.activation(out=s[:, :], in_=a[:, :], func=AF.Silu), [accs[0]], [s])
            P.op("pool", lambda e, s=s, a=accs[1], j=j: e.tensor_tensor(out=gb[j][:, :], in0=s[:, :], in1=a[:, :], op=ALU.mult),
                 [s, accs[1]], [gb[j]])
        for o in range(8):
            ps = banks[nbk % 6]
            nbk += 1
            for j in range(22):
                P.op("pe", lambda e, j=j, o=o, ps=ps: e.matmul(ps[:, 0:TF], lhsT=wdn[:, j * 1024 + o * 128:j * 1024 + (o + 1) * 128],
                                                               rhs=gb[j][:, :], start=(j == 0), stop=(j == 21)), [wdn, gb[j]], [ps])
            P.op("dve", lambda e, o=o, ps=ps: e.scalar_tensor_tensor(out=z[o][:, :], in0=x1c[o][:, 1:TF + 1], scalar=ALPHA, in1=ps[:, 0:TF],
                                                                     op0=ALU.mult, op1=ALU.add), [x1c[o], ps], [z[o]])
        outs = [oc[o % 2] for o in range(8)]

        def store(o, c0=c0, t=t):
            P.dma("sp", x2T[o * 128:(o + 1) * 128, c0:c0 + TF], oc[o % 2][:, :], reads=[oc[o % 2]], writes=[x2T.r((o, t // 2))],
                  semkey=oc[o % 2].name)
        emit_layernorm(P, PS, z, onesLN, vec, 0, 8, tmp, outs, ncols=TF, store=store)

    P.barrier()
    ar = Arena([wdn.t[:, :], wbig.t[:, 8 * NC_EXT:8 * 2 * D_FF]])
    W = emit_phaseA_setup(P, d, alloc=ar.alloc)
    xf = [ar.alloc("xf%d" % i, [128, T], F32) for i in range(2)]
    xb = [[ar.alloc("xb%d_%d" % (s, k), [128, T], BF16) for k in range(8)] for s in range(1)] * 2
    for t in range(NT):
        for k in range(8):
            f = xf[k % 2]
            P.dma("sp", f[:, :], x2T[k * 128:(k + 1) * 128, t * T:(t + 1) * T], reads=[x2T.r((k, t))], writes=[f])
            P.op("dve" if k % 2 == 0 else "pool", lambda e, f=f, k=k, t=t: e.tensor_copy(out=xb[t % 2][k][:, :], in_=f[:, :]),
                 [f], [xb[t % 2][k]])
        emit_phaseA_tile(P, PS, W, d, t, xb[t % 2], OA)
    P.finish("sp", OA.allres() + x2T.allres())
    P.emit()
    return nc


_PROGS = {}


def _prog(name):
    if name not in _PROGS:
        _PROGS[name] = {"A": build_A, "BC1": build_BC1, "C2A": build_C2A}[name]()
    return _PROGS[name]


def _run(name, in_maps):
    res = run_bass_kernel_spmd(_prog(name), in_maps, core_ids=list(range(NCORE)))
    return res.results


def _rope_perm(ncols, group, rot):
    half = rot // 2
    idx = np.arange(ncols)
    j = idx % group
    out = idx.copy()
    out[j < half] += half
    out[(j >= half) & (j < rot)] -= half
    return out


def _w_ext(w_in_l):
    a, b, c = 768, 1536, 1792
    dq, dk, dv = w_in_l[:, 0:256], w_in_l[:, 256:512], w_in_l[:, 512:768]
    bq, bk, bv = w_in_l[:, 768:1024], w_in_l[:, 1024:1280], w_in_l[:, 1280:1536]
    u = w_in_l[:, 1536:1792]
    cq, ckv, kr = w_in_l[:, 1792:2048], w_in_l[:, 2048:2176], w_in_l[:, 2176:2208]
    pd = _rope_perm(256, 32, 8)
    pb = _rope_perm(256, 64, 16)
    pk = _rope_perm(32, 32, 32)
    pad = np.zeros((1024, 64), np.float32)
    return np.ascontiguousarray(np.concatenate(
        [dq, dk, dq[:, pd], dk[:, pd], dv, bq, bk, bq[:, pb], bk[:, pb], bv, u, cq, ckv, kr, kr[:, pk], pad], axis=1))


def _wuq_ext(w_uq_l):
    cols = []
    pm = _rope_perm(32, 32, 32)
    for h in range(4):
        blk = w_uq_l[:, 96 * h:96 * h + 96]
        sw = np.concatenate([blk[:, 0:64], blk[:, 64:96][:, pm]], axis=1)
        cols += [blk, sw]
    return np.ascontiguousarray(np.concatenate(cols, axis=1))


def _rope_rows(n, group, rot, offset=0):
    half = rot // 2
    f = np.zeros(128, np.float32)
    s = np.zeros(128, np.float32)
    inv = (THETA ** (-(np.arange(half, dtype=np.float32) * 2.0 / rot))).astype(np.float32)
    for r in range(n):
        j = r % group - offset
        if 0 <= j < half:
            f[r], s[r] = inv[j], -1.0
        elif half <= j < rot:
            f[r], s[r] = inv[j - half], 1.0
    return f, s


def _vecA(q_norm_l, kv_norm_l):
    v = np.zeros((128, 16), np.float32)
    for i, (n, group, rot, off) in enumerate(((128, 32, 8, 0), (128, 64, 16, 0), (96, 96, 32, 64), (32, 32, 32, 0))):
        v[:, 2 * i], v[:, 2 * i + 1] = _rope_rows(n, group, rot, off)
    v[:, 8] = q_norm_l[0:128]
    v[:, 9] = q_norm_l[128:256]
    v[:, 10] = kv_norm_l
    return v


def _phaseA_inputs(inp, l, c):
    b, i = divmod(c, 4)
    return {"w_ext": _w_ext(inp["w_in"][l]), "wuq_ext": _wuq_ext(inp["mla_w_uq"][l]),
            "w_ukv": np.ascontiguousarray(inp["mla_w_ukv"][l]), "vecA": _vecA(inp["mla_q_norm"][l], inp["mla_kv_norm"][l]),
            "pos": np.ascontiguousarray(inp["positions"][b, i * TOK:(i + 1) * TOK].reshape(1, TOK))}


def _vext(vT):
    n = vT.shape[1]
    e = np.ones((n, 128), NPBF)
    e[:, 0:64] = vT.T
    return e


def _blocked(e):
    n = e.shape[0]
    return np.ascontiguousarray(e.reshape(n // 128, 128, e.shape[1]).transpose(1, 0, 2).reshape(128, -1))


_CONST = {}


def _consts():
    if _CONST:
        return _CONST
    dd = np.arange(-1408, 2944 - 1408 + 0)
    p = np.arange(128)[:, None]
    c = np.arange(2944)[None, :]
    dlt = p - c + 1408
    ad = np.abs(dlt)
    m = (ad <= 64).astype(np.float32) + ((dlt % 4 == 0) & (ad <= 256)) + ((dlt % 16 == 0) & (ad <= 1024))
    _CONST["dstrip"] = m.astype(NPBF)
    c = np.arange(1152)[None, :]
    dlt = p - c + 512
    ps = []
    for w in (2, 4, 8, 16):
        ps.append(((dlt >= -(w // 2)) & (dlt <= w // 2 - 1)).astype(np.float32))
    _CONST["pstrip"] = np.ascontiguousarray(np.concatenate(ps, axis=1).astype(NPBF))
    tt = np.arange(SEQ)
    ic = np.zeros((4, SEQ), np.float32)
    for g, w in enumerate((2, 4, 8, 16)):
        lo = np.clip(tt - w // 2, 0, SEQ - 1)
        hi = np.clip(tt + w - w // 2 - 1, 0, SEQ - 1)
        ic[g] = 1.0 / (hi - lo + 1).astype(np.float32)
    _CONST["invc"] = ic
    return _CONST


def _bc1_inputs(inp, l, OAs, xTs):
    cst = _consts()
    maps = []
    lam_init = 0.8 - 0.6 * math.exp(-0.3 * l)
    vecB = np.zeros((128, 160), np.float32)
    vecB[:, 0:8] = inp["ln1_g"][l].reshape(8, 128).T
    vecB[:, 8:16] = inp["ln1_b"][l].reshape(8, 128).T
    vecB[0:64, 16:20] = inp["pool_scale"][l].reshape(4, 64).T
    vecB[0:64, 20] = inp["diff_subln"][l]
    vecB[:, 21] = lam_init
    vecB[:, 22] = 1.0 - lam_init
    vecB[:, 32:160] = inp["diff_lambda"][l].reshape(1, 128)
    pool_w = np.ascontiguousarray(inp["pool_w"][l].transpose(1, 0, 2).reshape(64, 256))
    w_out = np.ascontiguousarray(inp["w_out"][l])
    for b in range(2):
        OAb = np.concatenate([OAs[4 * b + i] for i in range(4)], axis=1)
        dKT = np.ascontiguousarray(OAb[R_DK:R_DK + 256])
        dVx = np.stack([_blocked(_vext(OAb[R_DV + 64 * h:R_DV + 64 * h + 64])) for h in range(4)])
        mKT = np.stack([np.concatenate([OAb[R_MKV + 128 * h:R_MKV + 128 * h + 64], OAb[R_MKR:R_MKR + 32]], axis=0) for h in range(4)])
        mVx = np.stack([_blocked(_vext(OAb[R_MKV + 128 * h + 64:R_MKV + 128 * h + 128])) for h in range(4)])
        bKp = np.zeros((256, SEQ + 2048), NPBF)
        bKp[:, 1024:1024 + SEQ] = OAb[R_BK:R_BK + 256]
        bVp = []
        for h in range(4):
            e = np.zeros((SEQ + 2048, 128), NPBF)
            e[1024:1024 + SEQ] = _vext(OAb[R_BV + 64 * h:R_BV + 64 * h + 64])
            bVp.append(e)
        utp = np.zeros((SEQ + 256, 256), NPBF)
        utp[128:128 + SEQ] = OAb[R_U:R_U + 256].T
        for i in range(4):
            c = 4 * b + i
            OAc = OAs[c]
            m = {
                "QT": np.ascontiguousarray(np.concatenate([OAc[R_DQ:R_DQ + 256], OAc[R_BQ:R_BQ + 256], OAc[R_MQ:R_MQ + 384]], axis=0)),
                "dKT": dKT, "dVx": dVx, "mKT": np.ascontiguousarray(mKT), "mVx": mVx,
                "bKT": np.ascontiguousarray(bKp[:, TOK * i:TOK * i + 6144]),
                "bVx": np.stack([_blocked(bVp[h][TOK * i:TOK * i + 6144]) for h in range(4)]),
                "uTok": _blocked(utp[TOK * i:TOK * i + 34 * 128]),
                "u16T": np.ascontiguousarray(OAc[R_U:R_U + 256]),
                "invc": np.ascontiguousarray(np.broadcast_to(cst["invc"][:, None, TOK * i:TOK * (i + 1)], (4, 64, TOK))),
                "dstrip": cst["dstrip"], "pstrip": cst["pstrip"],
                "xT": xTs[c], "w_out": w_out, "pool_w": pool_w, "vecB": vecB,
            }
            maps.append(m)
    return maps


def _c2a_inputs(inp, l, lnext, x1Ts):
    vecC = np.zeros((128, 192), np.float32)
    vecC[:, 0:8] = inp["ln2_g"][l].reshape(8, 128).T
    vecC[:, 8:16] = inp["ln2_b"][l].reshape(8, 128).T
    for j in range(3):
        vecC[:, 16 + 44 * j:16 + 44 * (j + 1)] = inp["ffn_conv_w"][l][j].reshape(44, 128).T
    vecC[:, 148:192] = inp["ffn_conv_b"][l].reshape(44, 128).T
    w_up = np.ascontiguousarray(inp["ffn_w_up"][l])
    w_dn = np.ascontiguousarray(inp["ffn_w_down"][l])
    maps = []
    for b in range(2):
        xb = np.zeros((1024, SEQ + 2), np.float32)
        xb[:, 1:1 + SEQ] = np.concatenate([x1Ts[4 * b + i] for i in range(4)], axis=1)
        for i in range(4):
            c = 4 * b + i
            m = _phaseA_inputs(inp, lnext, c)
            m.update({"x1h": np.ascontiguousarray(xb[:, TOK * i:TOK * i + TOK + 2]), "w_up": w_up, "w_dn": w_dn, "vecC": vecC})
            maps.append(m)
    return maps


def kernel(**inp):
    inp = {k: np.asarray(v) for k, v in inp.items()}
    x = inp["x"]
    xTs = [np.ascontiguousarray(x[c // 4, (c % 4) * TOK:(c % 4 + 1) * TOK, :].T) for c in range(NCORE)]
    maps = []
    for c in range(NCORE):
        m = _phaseA_inputs(inp, 0, c)
        m["xT"] = xTs[c]
        maps.append(m)
    r = _run("A", maps)
    OAs = [np.asarray(r[c]["OA"]) for c in range(NCORE)]
    for l in range(DEPTH):
        r = _run("BC1", _bc1_inputs(inp, l, OAs, xTs))
        x1Ts = [np.asarray(r[c]["x1T"]) for c in range(NCORE)]
        r = _run("C2A", _c2a_inputs(inp, l, (l + 1) % DEPTH, x1Ts))
        xTs = [np.asarray(r[c]["x2T"]) for c in range(NCORE)]
        OAs = [np.asarray(r[c]["OA"]) for c in range(NCORE)]
    out = np.zeros((2, SEQ, D_MODEL), np.float32)
    for c in range(NCORE):
        out[c // 4, (c % 4) * TOK:(c % 4 + 1) * TOK, :] = xTs[c].T
    return out
```

```python
import math
import numpy as np
import ml_dtypes
import concourse.bass as bass
import concourse.mybir as mybir
from concourse.bass_utils import run_bass_kernel_spmd

F32 = mybir.dt.float32
BF16 = mybir.dt.bfloat16
I32 = mybir.dt.int32
ALU = mybir.AluOpType
AF = mybir.ActivationFunctionType
NPBF = ml_dtypes.bfloat16

D_MODEL = 1024
SEQ = 16384
DEPTH = 4
NCORE = 8
TOK = 4096
T = 512
NT = TOK // T
D_FF = 2816
ALPHA = (2 * DEPTH) ** 0.25
LN_EPS = 1e-5
RMS_EPS = 1e-6
THETA = 500000.0
TWO_PI = 2.0 * math.pi
CW1 = 6.28125
CW2 = TWO_PI - 6.28125
PI_LO = 3.1415925

R_DQ, R_DK, R_DV, R_BQ, R_BK, R_BV, R_U, R_MQ, R_MKV, R_MKR = 0, 256, 512, 768, 1024, 1280, 1536, 1792, 2176, 2688
NR_OA = 2720
NC_EXT = 26 * 128

COMPUTE = ("pe", "act", "dve", "pool")


class Res:
    __slots__ = ("name", "t", "last_write", "reads")

    def __init__(self, name, t=None):
        self.name = name
        self.t = t
        self.last_write = None
        self.reads = []

    def __getitem__(self, idx):
        return self.t[idx]


class Prog:
    def __init__(self, nc):
        self.nc = nc
        self.ops = {e: [] for e in COMPUTE + ("sp",)}
        self.cnt = {e: 0 for e in COMPUTE}
        self.esem = {e: nc.alloc_semaphore("s_" + e) for e in COMPUTE}
        self.dsems = {}
        self.waited = {e: {} for e in COMPUTE + ("sp",)}
        self.nops = 0

    def sb(self, name, shape, dt):
        return Res(name, self.nc.alloc_sbuf_tensor("sb_" + name, list(shape), dt))

    def ps(self, name, shape, dt=F32):
        return Res(name, self.nc.alloc_psum_tensor("ps_" + name, list(shape), dt))

    def dram(self, name, shape, dt, kind="Internal"):
        return Res(name, self.nc.dram_tensor(name, list(shape), dt, kind=kind))

    def _dsem(self, key):
        if key not in self.dsems:
            self.dsems[key] = [self.nc.alloc_semaphore("d_%d" % len(self.dsems)), 0, 0]
        return self.dsems[key]

    def _collect(self, eng, reads, writes):
        deps = []
        for r in reads:
            if r.last_write is not None:
                deps.append(r.last_write)
        for w in writes:
            if w.last_write is not None:
                deps.append(w.last_write)
            deps.extend(w.reads)
        waits = {}
        for kind, key, val, teng in deps:
            if kind == "c":
                if teng == "pe" and eng == "pe":
                    continue
                waits[("c", key)] = max(waits.get(("c", key), 0), val)
            else:
                st = self.dsems[key]
                st[2] = st[1]
                waits[("d", key)] = max(waits.get(("d", key), 0), st[1])
        out = []
        wd = self.waited[eng]
        for k, v in waits.items():
            if wd.get(k, 0) >= v:
                continue
            wd[k] = v
            out.append((self.esem[k[1]] if k[0] == "c" else self.dsems[k[1]][0], v))
        return out

    def _mark(self, tok, reads, writes):
        for r in reads:
            r.reads.append(tok)
        for w in writes:
            w.last_write = tok
            w.reads = []

    def op(self, eng, fn, reads=(), writes=()):
        reads, writes = list(reads), list(writes)
        waits = self._collect(eng, reads, writes)
        self.cnt[eng] += 1
        tok = ("c", eng, self.cnt[eng], eng)
        self.ops[eng].append((waits, fn, (self.esem[eng], 1)))
        self._mark(tok, reads, writes)
        self.nops += 1 + len(waits)

    def dma(self, q, out_ap, in_ap, reads=(), writes=(), semkey=None):
        reads, writes = list(reads), list(writes)
        if semkey is None:
            semkey = writes[0].name if (writes and writes[0].t is not None and not writes[0].name.startswith("D:")) else (
                reads[0].name if reads else writes[0].name)
        st = self._dsem(semkey)
        waits = self._collect(q, reads, writes)
        if st[2] > 0 and self.waited[q].get(("d", semkey), 0) < st[2]:
            waits.append((st[0], st[2]))
            self.waited[q][("d", semkey)] = st[2]
        st[1] += 16
        tok = ("d", semkey, st[1], q)

        def fn(e, out_ap=out_ap, in_ap=in_ap):
            return e.dma_start(out=out_ap, in_=in_ap)
        self.ops[q].append((waits, fn, (st[0], 16)))
        self._mark(tok, reads, writes)
        self.nops += 1 + len(waits)

    def barrier(self):
        allw = [(("c", e), self.cnt[e]) for e in COMPUTE if self.cnt[e] > 0]
        for key, st in self.dsems.items():
            if st[1] > 0:
                st[2] = st[1]
                allw.append((("d", key), st[1]))
        for eng in COMPUTE + ("sp",):
            waits = []
            wd = self.waited[eng]
            for k, v in allw:
                if wd.get(k, 0) >= v:
                    continue
                wd[k] = v
                waits.append((self.esem[k[1]] if k[0] == "c" else self.dsems[k[1]][0], v))
            self.ops[eng].append((waits, None, None))
            self.nops += len(waits)

    def finish(self, eng, resources):
        waits = self._collect(eng, list(resources), [])
        self.ops[eng].append((waits, None, None))

    def emit(self):
        with self.nc.Block() as block:
            def run(name):
                def body(e):
                    for waits, fn, inc in self.ops[name]:
                        for sem, v in waits:
                            e.wait_ge(sem, v)
                        if fn is not None:
                            fn(e).then_inc(inc[0], inc[1])
                return body
            block.tensor(run("pe"))
            block.scalar(run("act"))
            block.vector(run("dve"))
            block.gpsimd(run("pool"))
            block.sync(run("sp"))


class Arena:
    def __init__(self, tensors):
        self.chunks = [[t, 0, t.shape[1] * 2] for t in tensors]

    def alloc(self, name, shape, dt):
        nbytes = shape[1] * (4 if dt in (F32, I32) else 2)
        nbytes = (nbytes + 63) // 64 * 64
        for ch in self.chunks:
            if ch[1] + nbytes <= ch[2]:
                off = ch[1]
                ch[1] += nbytes
                v = ch[0][0:shape[0], off // 2:(off + nbytes) // 2]
                if dt != BF16:
                    v = v.bitcast(dt)
                return Res(name, v[:, 0:shape[1]])
        raise RuntimeError("arena full: " + name)


class DR:
    def __init__(self, P, name, shape, dt, kind="Internal"):
        self.t = P.nc.dram_tensor(name, list(shape), dt, kind=kind)
        self.name = name
        self.regs = {}
        self.whole = Res("D:" + name)

    def r(self, key=None):
        if key is None:
            return self.whole
        if key not in self.regs:
            self.regs[key] = Res("D:%s:%s" % (self.name, str(key)))
        return self.regs[key]

    def __getitem__(self, idx):
        return self.t[idx]

    def allres(self):
        return [self.whole] + list(self.regs.values())


def bcast_rows(dr, col0, ncols, nparts=128):
    return bass.AP(dr.t, col0, [[0, nparts], [1, ncols]])


def alloc_psum(P):
    return {n: P.ps(n, [128, 512]) for n in ("S0", "S1", "S2", "ACC0", "ACC1", "M0", "M1", "M2")}


def emit_rstd(P, out, src, eps_t, m, ncols):
    P.op("act", lambda e: e.activation(out=out[0:m, 0:ncols], in_=src[0:m, 0:ncols], func=AF.Sqrt, bias=eps_t[0:m, 0:1], scale=1.0),
         [src, eps_t], [out])
    P.op("dve", lambda e: e.reciprocal(out=out[0:m, 0:ncols], in_=out[0:m, 0:ncols]), [out], [out])


def make_eps(P, name, val, alloc=None):
    t = (alloc or P.sb)(name, [128, 1], F32)
    P.op("pool", lambda e: e.memset(t[:, :], val), [], [t])
    return t


def emit_layernorm(P, PS, z, onesLN, vec, gcol, bcol, tmp, outs, ncols=T, store=None):
    mps, vps = PS["M0"], PS["M1"]
    mean_sb, rstd_sb, sq = tmp["mean"], tmp["rstd"], tmp["sq"]
    for o in range(8):
        P.op("pe", lambda e, o=o: e.matmul(mps[:, 0:ncols], lhsT=onesLN[:, :], rhs=z[o][:, 0:ncols],
                                           start=(o == 0), stop=(o == 7)), [onesLN, z[o]], [mps])
    P.op("act", lambda e: e.activation(out=mean_sb[:, 0:ncols], in_=mps[:, 0:ncols], func=AF.Copy), [mps], [mean_sb])
    for o in range(8):
        eng = "dve" if o % 2 == 0 else "pool"
        P.op(eng, lambda e, o=o: e.tensor_tensor(out=z[o][:, 0:ncols], in0=z[o][:, 0:ncols], in1=mean_sb[:, 0:ncols],
                                                 op=ALU.subtract), [z[o], mean_sb], [z[o]])
        s = sq[o % 2]
        P.op("act", lambda e, o=o, s=s: e.activation(out=s[:, 0:ncols], in_=z[o][:, 0:ncols], func=AF.Square), [z[o]], [s])
        P.op("pe", lambda e, o=o, s=s: e.matmul(vps[:, 0:ncols], lhsT=onesLN[:, :], rhs=s[:, 0:ncols],
                                                start=(o == 0), stop=(o == 7)), [onesLN, s], [vps])
    emit_rstd(P, rstd_sb, vps, tmp["eps"], 128, ncols)
    for o in range(8):
        eng = "dve" if o % 2 == 1 else "pool"
        P.op(eng, lambda e, o=o: e.tensor_tensor(out=z[o][:, 0:ncols], in0=z[o][:, 0:ncols], in1=rstd_sb[:, 0:ncols],
                                                 op=ALU.mult), [z[o], rstd_sb], [z[o]])
        P.op("act", lambda e, o=o: e.activation(out=outs[o][:, 0:ncols], in_=z[o][:, 0:ncols], func=AF.Identity,
                                                scale=vec[:, gcol + o:gcol + o + 1], bias=vec[:, bcol + o:bcol + o + 1]),
             [z[o], vec], [outs[o]])
        if store is not None:
            store(o)


def emit_phaseA_setup(P, d, alloc=None):
    W = {}
    if alloc is None:
        alloc = P.sb
    class _A:
        sb = staticmethod(alloc)
    PA = _A
    W["wA"] = d["wbig"]
    wA = W["wA"]
    for k in range(8):
        P.dma("pool", wA.t[:, k * NC_EXT:(k + 1) * NC_EXT], d["w_ext"][k * 128:(k + 1) * 128, :],
              reads=[d["w_ext"].r()], writes=[wA], semkey="wA")
    W["wuq"] = PA.sb("wuq", [128, 2 * 768], BF16)
    for k in range(2):
        P.dma("pool", W["wuq"][:, k * 768:(k + 1) * 768], d["wuq_ext"][k * 128:(k + 1) * 128, :], reads=[d["wuq_ext"].r()],
              writes=[W["wuq"]], semkey="wuq")
    W["wukv"] = PA.sb("wukv", [128, 512], BF16)
    P.dma("pool", W["wukv"][:, :], d["w_ukv"][:, :], reads=[d["w_ukv"].r()], writes=[W["wukv"]], semkey="wukv")
    W["vecA"] = PA.sb("vecA", [128, 16], F32)
    P.dma("sp", W["vecA"][:, :], d["vecA"][:, :], reads=[d["vecA"].r()], writes=[W["vecA"]])
    W["onesq"] = PA.sb("onesq", [128, 128], F32)
    P.op("pool", lambda e: e.memset(W["onesq"][:, :], 1.0 / 256.0), [], [W["onesq"]])
    W["oneskv"] = PA.sb("oneskv", [128, 128], F32)
    P.op("pool", lambda e: e.memset(W["oneskv"][:, :], 1.0 / 128.0), [], [W["oneskv"]])
    W["posi"] = PA.sb("posi", [128, T], I32)
    W["posf"] = PA.sb("posf", [128, T], F32)
    W["ang"] = PA.sb("ang", [128, T], F32)
    W["r1"] = PA.sb("r1", [128, T], F32)
    W["r2"] = PA.sb("r2", [128, T], F32)
    W["kf"] = PA.sb("kf", [128, T], F32)
    W["ki"] = PA.sb("ki", [128, T], I32)
    W["C"] = [PA.sb("ropeC%d" % i, [128, T], F32) for i in range(4)]
    W["S"] = [PA.sb("ropeS%d" % i, [128, T], F32) for i in range(4)]
    W["t1"] = [PA.sb("ra_t1_%d" % i, [128, T], F32) for i in range(2)]
    W["t2"] = [PA.sb("ra_t2_%d" % i, [128, T], F32) for i in range(2)]
    W["ob"] = [PA.sb("ra_ob_%d" % i, [128, T], BF16) for i in range(3)]
    W["cq"] = [PA.sb("cq%d" % i, [128, T], F32) for i in range(3)]
    W["cqsq"] = [PA.sb("cqsq%d" % i, [128, T], F32) for i in range(2)]
    W["rs"] = PA.sb("rs_rstd", [128, T], F32)
    W["cqn"] = [PA.sb("cqn%d" % i, [128, T], BF16) for i in range(3)]
    W["epsr"] = make_eps(P, "epsr_A", RMS_EPS, alloc)
    W["pibias"] = PA.sb("pibias", [128, 1], F32)
    P.op("pool", lambda e: e.memset(W["pibias"][:, :], PI_LO), [], [W["pibias"]])
    return W


def emit_phaseA_tile(P, PS, W, d, t, xb, OA):
    wA, vecA = W["wA"], W["vecA"]
    c0 = t * T
    banks = [PS["S0"], PS["S1"], PS["S2"], PS["ACC0"], PS["ACC1"], PS["M2"]]
    st = {"b": 0, "ob": 0, "t": 0}

    def nb():
        b = banks[st["b"] % len(banks)]
        st["b"] += 1
        return b

    def nob():
        b = W["ob"][st["ob"] % 3]
        st["ob"] += 1
        return b

    def proj(ps, col0, m, rows=128):
        for k in range(8):
            P.op("pe", lambda e, k=k: e.matmul(ps[0:m, :], lhsT=wA.t[0:128, k * NC_EXT + col0:k * NC_EXT + col0 + m],
                                               rhs=xb[k][:, :], start=(k == 0), stop=(k == 7)), [wA, xb[k]], [ps])

    def store(ob, m, row0):
        P.dma("sp", OA[row0:row0 + m, c0:c0 + T], ob[0:m, :], reads=[ob], writes=[OA.r((row0, t))], semkey=ob.name)

    P.dma("sp", W["posi"][:, :], bcast_rows(d["pos"], c0, T), reads=[d["pos"].r()], writes=[W["posi"]])
    P.op("dve", lambda e: e.tensor_copy(out=W["posf"][:, :], in_=W["posi"][:, :]), [W["posi"]], [W["posf"]])
    for i in range(4):
        m = (128, 128, 96, 32)[i]
        ang, r1, r2, kf, ki = W["ang"], W["r1"], W["r2"], W["kf"], W["ki"]
        P.op("dve", lambda e, i=i, m=m: e.tensor_scalar(out=ang[0:m, :], in0=W["posf"][0:m, :],
                                                        scalar1=vecA[0:m, 2 * i:2 * i + 1], scalar2=None, op0=ALU.mult),
             [W["posf"], vecA], [ang])
        P.op("dve", lambda e, m=m: e.tensor_scalar(out=kf[0:m, :], in0=ang[0:m, :], scalar1=1.0 / TWO_PI, scalar2=None, op0=ALU.mult),
             [ang], [kf])
        P.op("dve", lambda e, m=m: e.tensor_copy(out=ki[0:m, :], in_=kf[0:m, :]), [kf], [ki])
        P.op("dve", lambda e, m=m: e.tensor_copy(out=kf[0:m, :], in_=ki[0:m, :]), [ki], [kf])
        P.op("dve", lambda e, m=m: e.scalar_tensor_tensor(out=r1[0:m, :], in0=kf[0:m, :], scalar=-CW1, in1=ang[0:m, :],
                                                          op0=ALU.mult, op1=ALU.add), [kf, ang], [r1])
        P.op("dve", lambda e, m=m: e.scalar_tensor_tensor(out=r1[0:m, :], in0=kf[0:m, :], scalar=-CW2, in1=r1[0:m, :],
                                                          op0=ALU.mult, op1=ALU.add), [kf, r1], [r1])
        P.op("dve", lambda e, m=m: e.tensor_scalar(out=r2[0:m, :], in0=r1[0:m, :], scalar1=0.5 * math.pi, scalar2=None, op0=ALU.add),
             [r1], [r2])
        for rr in (r1, r2):
            for thr, cmp_, per in ((PI_LO, ALU.is_gt, -TWO_PI), (-PI_LO, ALU.is_lt, TWO_PI)):
                P.op("dve", lambda e, m=m, rr=rr, thr=thr, cmp_=cmp_, per=per: e.tensor_scalar(
                    out=ang[0:m, :], in0=rr[0:m, :], scalar1=thr, scalar2=per, op0=cmp_, op1=ALU.mult), [rr], [ang])
                P.op("dve", lambda e, m=m, rr=rr: e.tensor_tensor(out=rr[0:m, :], in0=rr[0:m, :], in1=ang[0:m, :], op=ALU.add),
                     [rr, ang], [rr])
            P.op("dve", lambda e, m=m, rr=rr: e.tensor_scalar(out=rr[0:m, :], in0=rr[0:m, :], scalar1=-PI_LO, scalar2=PI_LO,
                                                              op0=ALU.max, op1=ALU.min), [rr], [rr])
        P.op("act", lambda e, i=i, m=m: e.activation(out=W["S"][i][0:m, :], in_=r1[0:m, :], func=AF.Sin), [r1], [W["S"][i]])
        P.op("act", lambda e, i=i, m=m: e.activation(out=W["C"][i][0:m, :], in_=r2[0:m, :], func=AF.Sin), [r2], [W["C"][i]])
        P.op("dve", lambda e, i=i, m=m: e.tensor_scalar(out=W["S"][i][0:m, :], in0=W["S"][i][0:m, :],
                                                        scalar1=vecA[0:m, 2 * i + 1:2 * i + 2], scalar2=None, op0=ALU.mult),
             [W["S"][i], vecA], [W["S"][i]])

    def rope_out(pa, pb, m, pat, row0):
        i = st["t"] % 2
        st["t"] += 1
        t1, t2 = W["t1"][i], W["t2"][i]
        ob = nob()
        P.op("dve", lambda e: e.tensor_tensor(out=t1[0:m, :], in0=pa[0:m, :], in1=W["C"][pat][0:m, :], op=ALU.mult),
             [pa, W["C"][pat]], [t1])
        P.op("dve", lambda e: e.tensor_tensor(out=t2[0:m, :], in0=pb[0:m, :], in1=W["S"][pat][0:m, :], op=ALU.mult),
             [pb, W["S"][pat]], [t2])
        P.op("pool", lambda e: e.tensor_tensor(out=ob[0:m, :], in0=t1[0:m, :], in1=t2[0:m, :], op=ALU.add), [t1, t2], [ob])
        store(ob, m, row0)

    def plain_out(pa, m, row0):
        ob = nob()
        P.op("act", lambda e: e.activation(out=ob[0:m, :], in_=pa[0:m, :], func=AF.Copy), [pa], [ob])
        store(ob, m, row0)

    for j, row0 in enumerate((R_DQ, R_DQ + 128, R_DK, R_DK + 128)):
        pa, pb = nb(), nb()
        proj(pa, j * 128, 128)
        proj(pb, (4 + j) * 128, 128)
        rope_out(pa, pb, 128, 0, row0)
    for j, row0 in enumerate((R_DV, R_DV + 128)):
        pa = nb()
        proj(pa, (8 + j) * 128, 128)
        plain_out(pa, 128, row0)
    for j, row0 in enumerate((R_BQ, R_BQ + 128, R_BK, R_BK + 128)):
        pa, pb = nb(), nb()
        proj(pa, (10 + j) * 128, 128)
        proj(pb, (14 + j) * 128, 128)
        rope_out(pa, pb, 128, 1, row0)
    for j, row0 in enumerate((R_BV, R_BV + 128, R_U, R_U + 128)):
        pa = nb()
        proj(pa, (18 + j) * 128, 128)
        plain_out(pa, 128, row0)
    pa, pb = nb(), nb()
    proj(pa, 25 * 128, 32)
    proj(pb, 25 * 128 + 32, 32)
    rope_out(pa, pb, 32, 3, R_MKR)
    for j in range(3):
        pa = nb()
        proj(pa, (22 + j) * 128, 128)
        P.op("act", lambda e, j=j, pa=pa: e.activation(out=W["cq"][j][:, :], in_=pa[:, :], func=AF.Copy), [pa], [W["cq"][j]])
    for grp, (chunks, ones, eps_col) in enumerate((((0, 1), W["onesq"], 8), ((2,), W["oneskv"], 10))):
        msp = nb()
        for n, j in enumerate(chunks):
            s = W["cqsq"][n % 2]
            P.op("act", lambda e, j=j, s=s: e.activation(out=s[:, :], in_=W["cq"][j][:, :], func=AF.Square), [W["cq"][j]], [s])
            P.op("pe", lambda e, s=s, n=n, ones=ones, L=len(chunks), msp=msp: e.matmul(msp[:, :], lhsT=ones[:, :], rhs=s[:, :],
                                                                             start=(n == 0), stop=(n == L - 1)),
                 [ones, s], [msp])
        emit_rstd(P, W["rs"], msp, W["epsr"], 128, T)
        for j in chunks:
            P.op("dve", lambda e, j=j: e.tensor_tensor(out=W["cq"][j][:, :], in0=W["cq"][j][:, :], in1=W["rs"][:, :],
                                                       op=ALU.mult), [W["cq"][j], W["rs"]], [W["cq"][j]])
            P.op("dve", lambda e, j=j: e.tensor_scalar(out=W["cqn"][j][:, :], in0=W["cq"][j][:, :],
                                                       scalar1=vecA[:, 8 + j:9 + j], scalar2=None, op0=ALU.mult),
                 [W["cq"][j], vecA], [W["cqn"][j]])
    for h in range(4):
        pa, pb = nb(), nb()
        for which, ps in ((0, pa), (1, pb)):
            col0 = h * 192 + which * 96
            for k in range(2):
                P.op("pe", lambda e, k=k, ps=ps, col0=col0: e.matmul(ps[0:96, :], lhsT=W["wuq"][:, k * 768 + col0:k * 768 + col0 + 96],
                                                                     rhs=W["cqn"][k][:, :], start=(k == 0), stop=(k == 1)),
                     [W["wuq"], W["cqn"][k]], [ps])
        rope_out(pa, pb, 96, 2, R_MQ + 96 * h)
    for h in range(4):
        pa = nb()
        P.op("pe", lambda e, h=h, pa=pa: e.matmul(pa[:, :], lhsT=W["wukv"][:, h * 128:(h + 1) * 128], rhs=W["cqn"][2][:, :],
                                                  start=True, stop=True), [W["wukv"], W["cqn"][2]], [pa])
        plain_out(pa, 128, R_MKV + 128 * h)


def declare_phaseA_inputs(P):
    d = {}
    d["w_ext"] = DR(P, "w_ext", [1024, NC_EXT], F32, "ExternalInput")
    d["wuq_ext"] = DR(P, "wuq_ext", [256, 768], F32, "ExternalInput")
    d["w_ukv"] = DR(P, "w_ukv", [128, 512], F32, "ExternalInput")
    d["vecA"] = DR(P, "vecA", [128, 16], F32, "ExternalInput")
    d["pos"] = DR(P, "pos", [1, TOK], I32, "ExternalInput")
    return d


def build_A():
    nc = bass.Bass("TRN2", target_bir_lowering=False)
    P = Prog(nc)
    d = declare_phaseA_inputs(P)
    xT = DR(P, "xT", [1024, TOK], F32, "ExternalInput")
    OA = DR(P, "OA", [NR_OA, TOK], BF16, "ExternalOutput")
    PS = alloc_psum(P)
    d["wbig"] = P.sb("wbig", [128, 8 * NC_EXT], BF16)
    W = emit_phaseA_setup(P, d)
    xf = [P.sb("xf%d" % i, [128, T], F32) for i in range(2)]
    xb = [[P.sb("xb%d_%d" % (s, k), [128, T], BF16) for k in range(8)] for s in range(2)]
    for t in range(NT):
        for k in range(8):
            f = xf[k % 2]
            P.dma("sp", f[:, :], xT[k * 128:(k + 1) * 128, t * T:(t + 1) * T], reads=[xT.r()], writes=[f])
            P.op("dve" if k % 2 == 0 else "pool", lambda e, f=f, k=k, t=t: e.tensor_copy(out=xb[t % 2][k][:, :], in_=f[:, :]),
                 [f], [xb[t % 2][k]])
        emit_phaseA_tile(P, PS, W, d, t, xb[t % 2], OA)
    P.finish("sp", OA.allres())
    P.emit()
    return nc


def build_BC1():
    nc = bass.Bass("TRN2", target_bir_lowering=False)
    P = Prog(nc)
    QT = DR(P, "QT", [896, TOK], BF16, "ExternalInput")
    dKT = DR(P, "dKT", [256, SEQ], BF16, "ExternalInput")
    dVx = DR(P, "dVx", [4, 128, SEQ], BF16, "ExternalInput")
    bKT = DR(P, "bKT", [256, 6144], BF16, "ExternalInput")
    bVx = DR(P, "bVx", [4, 128, 6144], BF16, "ExternalInput")
    mKT = DR(P, "mKT", [4, 96, SEQ], BF16, "ExternalInput")
    mVx = DR(P, "mVx", [4, 128, SEQ], BF16, "ExternalInput")
    uTok = DR(P, "uTok", [128, 34 * 256], BF16, "ExternalInput")
    u16T = DR(P, "u16T", [256, TOK], BF16, "ExternalInput")
    invc = DR(P, "invc", [4, 64, TOK], F32, "ExternalInput")
    dstrip = DR(P, "dstrip", [128, 2944], BF16, "ExternalInput")
    pstrip = DR(P, "pstrip", [128, 4 * 1152], BF16, "ExternalInput")
    xT = DR(P, "xT", [1024, TOK], F32, "ExternalInput")
    w_out = DR(P, "w_out", [1024, 1024], F32, "ExternalInput")
    pool_w = DR(P, "pool_w", [64, 256], F32, "ExternalInput")
    vecB = DR(P, "vecB", [128, 160], F32, "ExternalInput")
    x1T = DR(P, "x1T", [1024, TOK], F32, "ExternalOutput")
    mixT = DR(P, "mixT", [1024, TOK], BF16)
    PS = alloc_psum(P)

    vec = P.sb("vecB", [128, 160], F32)
    P.dma("sp", vec[:, :], vecB[:, :], reads=[vecB.r()], writes=[vec])
    wo = P.sb("wo", [128, 8 * 1024], BF16)
    for k in range(8):
        P.dma("pool", wo[:, k * 1024:(k + 1) * 1024], w_out[k * 128:(k + 1) * 128, :], reads=[w_out.r()], writes=[wo], semkey="wo")
    pw = P.sb("pw", [64, 256], BF16)
    P.dma("pool", pw[:, :], pool_w[:, :], reads=[pool_w.r()], writes=[pw])
    ds = P.sb("dstrip", [128, 2944], BF16)
    P.dma("sp", ds[:, :], dstrip[:, :], reads=[dstrip.r()], writes=[ds])
    pst = P.sb("pstrip", [128, 4 * 1152], BF16)
    P.dma("sp", pst[:, :], pstrip[:, :], reads=[pstrip.r()], writes=[pst])
    ut = P.sb("utok", [128, 34 * 256], BF16)
    P.dma("sp", ut[:, :], uTok[:, :], reads=[uTok.r()], writes=[ut])
    ones64 = P.sb("ones64", [64, 64], F32)
    P.op("pool", lambda e: e.memset(ones64[:, :], 1.0 / 64.0), [], [ones64])
    onesLN = P.sb("onesLN", [128, 128], F32)
    P.op("pool", lambda e: e.memset(onesLN[:, :], 1.0 / 1024.0), [], [onesLN])
    lt = P.sb("lam_t", [128, 64], F32)
    ls = P.sb("lam_s", [128, 4], F32)
    P.op("dve", lambda e: e.tensor_tensor(out=lt[:, 0:32], in0=vec[:, 32:64], in1=vec[:, 64:96], op=ALU.mult), [vec], [lt])
    P.op("dve", lambda e: e.tensor_tensor(out=lt[:, 32:64], in0=vec[:, 96:128], in1=vec[:, 128:160], op=ALU.mult), [vec, lt], [lt])
    P.op("dve", lambda e: e.reduce_sum(out=ls[:, 0:1], in_=lt[:, 0:32], axis=mybir.AxisListType.X), [lt], [ls])
    P.op("dve", lambda e: e.reduce_sum(out=ls[:, 1:2], in_=lt[:, 32:64], axis=mybir.AxisListType.X), [lt, ls], [ls])
    P.op("act", lambda e: e.activation(out=ls[:, 0:2], in_=ls[:, 0:2], func=AF.Exp), [ls], [ls])
    P.op("dve", lambda e: e.tensor_tensor(out=ls[:, 2:3], in0=ls[:, 1:2], in1=ls[:, 0:1], op=ALU.subtract), [ls], [ls])
    P.op("dve", lambda e: e.tensor_tensor(out=ls[:, 2:3], in0=ls[:, 2:3], in1=vec[:, 21:22], op=ALU.subtract), [ls, vec], [ls])
    P.op("dve", lambda e: e.tensor_tensor(out=ls[:, 3:4], in0=vec[:, 20:21], in1=vec[:, 22:23], op=ALU.mult), [ls, vec], [ls])
    lam = ls

    epsr = make_eps(P, "epsr", RMS_EPS)
    KCH = 2048
    ktc = [P.sb("kt%d" % i, [128, KCH], BF16) for i in range(SEQ // KCH)]
    vxc = [P.sb("vx%d" % i, [128, KCH], BF16) for i in range(SEQ // KCH)]

    def load_kv(ksrc, krows, vsrc, nkeys):
        for ci in range(nkeys // KCH):
            P.dma("pool", ktc[ci][0:krows, :], ksrc(ci * KCH, (ci + 1) * KCH), reads=[QT.r()], writes=[ktc[ci]])
            P.dma("pool", vxc[ci][:, :], vsrc(ci * KCH, (ci + 1) * KCH), reads=[QT.r()], writes=[vxc[ci]])
    qb = [P.sb("q%d" % i, [128, T], BF16) for i in range(2)]
    pT = [P.sb("pT%d" % i, [128, T], BF16) for i in range(4)]
    pT2 = [P.sb("pTm%d" % i, [128, T], BF16) for i in range(4)]
    rden = [P.sb("rden%d" % i, [64, T], F32) for i in range(2)]
    o32 = [P.sb("o32_%d" % i, [64, T], F32) for i in range(2)]
    osq = P.sb("osq", [64, T], F32)
    orst = P.sb("orst", [64, T], F32)
    ob16 = [P.sb("ob16_%d" % i, [64, T], BF16) for i in range(2)]
    Sb = [PS["S0"], PS["S1"], PS["S2"]]
    cnt = {"s": 0, "p": 0, "pm": 0, "q": 0, "ob": 0}

    LOOK = 2

    def attn(d, pbase, nblk, blk0, scale, acc, q, strip_delta0=None):
        pend = []
        for i in range(nblk + LOOK):
            if i < nblk:
                kb = blk0 + i
                sps = Sb[cnt["s"] % 3]
                cnt["s"] += 1
                p = pT[cnt["p"] % 4]
                cnt["p"] += 1
                kt = ktc[kb // 16]
                ko = (kb % 16) * 128
                P.op("pe", lambda e, kt=kt, ko=ko, sps=sps: e.matmul(sps[:, :], lhsT=kt[pbase:pbase + d, ko:ko + 128],
                                                                     rhs=q[pbase:pbase + d, :], start=True, stop=True), [kt, q], [sps])
                P.op("act", lambda e, sps=sps, p=p: e.activation(out=p[:, :], in_=sps[:, :], func=AF.Exp, scale=scale), [sps], [p])
                if strip_delta0 is not None:
                    c0 = 1408 - 128 * (strip_delta0 + i)
                    p2 = pT2[cnt["pm"] % 4]
                    cnt["pm"] += 1
                    P.op("dve", lambda e, p=p, p2=p2, c0=c0: e.tensor_tensor(out=p2[:, :], in0=p[:, :], in1=ds[:, c0:c0 + T], op=ALU.mult),
                         [p, ds], [p2])
                    p = p2
                pend.append((kb, p, i))
            if i >= LOOK:
                kb, p, j = pend.pop(0)
                vx = vxc[kb // 16]
                vo = (kb % 16) * 128
                P.op("pe", lambda e, vx=vx, vo=vo, p=p, j=j: e.matmul(acc[:, :], lhsT=vx[:, vo:vo + 128], rhs=p[:, :],
                                                                      start=(j == 0), stop=(j == nblk - 1)), [vx, p], [acc])

    def normalize(acc, out32, rd):
        P.op("dve", lambda e: e.reciprocal(out=rd[0:64, :], in_=acc[64:128, :]), [acc], [rd])
        P.op("dve", lambda e: e.tensor_tensor(out=out32[0:64, :], in0=acc[0:64, :], in1=rd[0:64, :], op=ALU.mult), [acc, rd], [out32])

    def store_mix(ob, row0, t):
        P.dma("sp", mixT[row0:row0 + 64, t * T:(t + 1) * T], ob[0:64, :], reads=[ob], writes=[mixT.r((row0 // 128, t))], semkey=ob.name)

    def load_q(row0, d, pbase, t):
        q = qb[cnt["q"] % 2]
        cnt["q"] += 1
        P.dma("sp", q[pbase:pbase + d, :], QT[row0:row0 + d, t * T:(t + 1) * T], reads=[QT.r()], writes=[q])
        return q

    for h in range(4):
        load_kv(lambda a, b, h=h: dKT[64 * h:64 * h + 64, a:b], 64, lambda a, b, h=h: dVx[h, :, a:b], SEQ)
        for t in range(NT):
            q = load_q(R_DQ + 64 * h, 64, 0, t)
            attn(32, 0, 128, 0, 32 ** -0.5, PS["ACC0"], q)
            attn(32, 32, 128, 0, 32 ** -0.5, PS["ACC1"], q)
            normalize(PS["ACC0"], o32[0], rden[0])
            normalize(PS["ACC1"], o32[1], rden[1])
            o = o32[0]
            P.op("dve", lambda e, o=o: e.scalar_tensor_tensor(out=o[0:64, :], in0=o32[1][0:64, :], scalar=lam[0:64, 2:3], in1=o[0:64, :],
                                                              op0=ALU.mult, op1=ALU.add), [o32[1], o, lam], [o])
            P.op("act", lambda e, o=o: e.activation(out=osq[0:64, :], in_=o[0:64, :], func=AF.Square), [o], [osq])
            P.op("pe", lambda e: e.matmul(PS["M0"][0:64, :], lhsT=ones64[:, :], rhs=osq[0:64, :], start=True, stop=True),
                 [ones64, osq], [PS["M0"]])
            emit_rstd(P, orst, PS["M0"], epsr, 64, T)
            P.op("dve", lambda e, o=o: e.tensor_tensor(out=o[0:64, :], in0=o[0:64, :], in1=orst[0:64, :], op=ALU.mult), [o, orst], [o])
            ob = ob16[cnt["ob"] % 2]
            cnt["ob"] += 1
            P.op("dve", lambda e, o=o, ob=ob: e.tensor_scalar(out=ob[0:64, :], in0=o[0:64, :], scalar1=lam[0:64, 3:4], scalar2=None,
                                                              op0=ALU.mult), [o, lam], [ob])
            store_mix(ob, 64 * h, t)
    for h in range(4):
        load_kv(lambda a, b, h=h: mKT[h, :, a:b], 96, lambda a, b, h=h: mVx[h, :, a:b], SEQ)
        for t in range(NT):
            q = load_q(512 + 96 * h, 96, 0, t)
            acc = PS["ACC0"] if t % 2 == 0 else PS["ACC1"]
            attn(96, 0, 128, 0, 96 ** -0.5, acc, q)
            i = t % 2
            normalize(acc, o32[i], rden[i])
            ob = ob16[cnt["ob"] % 2]
            cnt["ob"] += 1
            P.op("act", lambda e, i=i, ob=ob: e.activation(out=ob[0:64, :], in_=o32[i][0:64, :], func=AF.Copy), [o32[i]], [ob])
            store_mix(ob, 768 + 64 * h, t)
    for h in range(4):
        load_kv(lambda a, b, h=h: bKT[64 * h:64 * h + 64, a:b], 64, lambda a, b, h=h: bVx[h, :, a:b], 6144)
        for t in range(NT):
            q = load_q(256 + 64 * h, 64, 0, t)
            acc = PS["ACC0"] if t % 2 == 0 else PS["ACC1"]
            attn(64, 0, 20, 4 * t, 64 ** -0.5, acc, q, strip_delta0=-8)
            i = t % 2
            normalize(acc, o32[i], rden[i])
            ob = ob16[cnt["ob"] % 2]
            cnt["ob"] += 1
            P.op("act", lambda e, i=i, ob=ob: e.activation(out=ob[0:64, :], in_=o32[i][0:64, :], func=AF.Copy), [o32[i]], [ob])
            store_mix(ob, 256 + 64 * h, t)
    ic = [P.sb("ic%d" % i, [64, T], F32) for i in range(2)]
    uu = [P.sb("uu%d" % i, [128, T], BF16) for i in range(2)]
    dd = [P.sb("dd%d" % i, [64, T], F32) for i in range(2)]
    db = [P.sb("db%d" % i, [64, T], BF16) for i in range(2)]
    n = 0
    for t in range(NT):
        for g in range(4):
            i = n % 2
            n += 1
            pp, py = PS["M1"], PS["M2"]
            for dl in range(-1, 5):
                blk = 4 * t + dl + 1
                c0 = g * 1152 + 512 - 128 * dl
                P.op("pe", lambda e, blk=blk, c0=c0, dl=dl, g=g: e.matmul(
                    pp[0:64, :], lhsT=ut[:, blk * 256 + 64 * g:blk * 256 + 64 * g + 64], rhs=pst[:, c0:c0 + T],
                    start=(dl == -1), stop=(dl == 4)), [ut, pst], [pp])
            P.dma("sp", ic[i][:, :], invc[g, :, t * T:(t + 1) * T], reads=[invc.r()], writes=[ic[i]])
            ub = uu[i]
            P.dma("sp", ub[0:64, :], u16T[64 * g:64 * g + 64, t * T:(t + 1) * T], reads=[u16T.r()], writes=[ub])
            pb0 = 0
            P.op("dve", lambda e, i=i: e.tensor_tensor(out=dd[i][0:64, :], in0=pp[0:64, :], in1=ic[i][0:64, :], op=ALU.mult),
                 [pp, ic[i]], [dd[i]])
            P.op("dve", lambda e, i=i, ub=ub, pb0=pb0: e.tensor_tensor(out=db[i][0:64, :], in0=dd[i][0:64, :], in1=ub[pb0:pb0 + 64, :],
                                                                       op=ALU.subtract), [dd[i], ub], [db[i]])
            P.op("pe", lambda e, i=i, g=g: e.matmul(py[0:64, :], lhsT=pw[0:64, 64 * g:64 * g + 64], rhs=db[i][0:64, :],
                                                    start=True, stop=True), [pw, db[i]], [py])
            ob = ob16[cnt["ob"] % 2]
            cnt["ob"] += 1
            P.op("dve", lambda e, ob=ob, g=g: e.tensor_scalar(out=ob[0:64, :], in0=py[0:64, :], scalar1=vec[0:64, 16 + g:17 + g],
                                                              scalar2=None, op0=ALU.mult), [py, vec], [ob])
            store_mix(ob, 512 + 64 * g, t)

    mx = [P.sb("mx%d" % k, [128, T], BF16) for k in range(8)]
    z = [P.sb("z%d" % k, [128, T], F32) for k in range(8)]
    xc = [P.sb("xc%d" % k, [128, T], F32) for k in range(2)]
    oc = [P.sb("oc%d" % k, [128, T], F32) for k in range(2)]
    tmp = {"mean": P.sb("ln_mean", [128, T], F32), "rstd": P.sb("ln_rstd", [128, T], F32),
           "sq": [P.sb("ln_sq%d" % k, [128, T], F32) for k in range(2)], "eps": make_eps(P, "epsln", LN_EPS)}
    for t in range(NT):
        for k in range(8):
            P.dma("sp", mx[k][:, :], mixT[k * 128:(k + 1) * 128, t * T:(t + 1) * T], reads=[mixT.r((k, t))], writes=[mx[k]])
        for o in range(8):
            yps = Sb[o % 3]
            for k in range(8):
                P.op("pe", lambda e, k=k, o=o, yps=yps: e.matmul(yps[:, :], lhsT=wo[:, k * 1024 + o * 128:k * 1024 + (o + 1) * 128],
                                                                 rhs=mx[k][:, :], start=(k == 0), stop=(k == 7)), [wo, mx[k]], [yps])
            x_ = xc[o % 2]
            P.dma("sp", x_[:, :], xT[o * 128:(o + 1) * 128, t * T:(t + 1) * T], reads=[xT.r()], writes=[x_])
            P.op("dve", lambda e, o=o, x_=x_, yps=yps: e.scalar_tensor_tensor(out=z[o][:, :], in0=x_[:, :], scalar=ALPHA, in1=yps[:, :],
                                                                              op0=ALU.mult, op1=ALU.add), [x_, yps], [z[o]])
        outs = [oc[o % 2] for o in range(8)]

        def store(o, t=t):
            P.dma("sp", x1T[o * 128:(o + 1) * 128, t * T:(t + 1) * T], oc[o % 2][:, :], reads=[oc[o % 2]], writes=[x1T.r((o, t))],
                  semkey=oc[o % 2].name)
        emit_layernorm(P, PS, z, onesLN, vec, 0, 8, tmp, outs, store=store)
    P.finish("sp", x1T.allres())
    P.emit()
    return nc


TF = 256
def build_C2A():
    nc = bass.Bass("TRN2", target_bir_lowering=False)
    P = Prog(nc)
    d = declare_phaseA_inputs(P)
    x1h = DR(P, "x1h", [1024, TOK + 2], F32, "ExternalInput")
    w_up = DR(P, "w_up", [1024, 2 * D_FF], F32, "ExternalInput")
    w_dn = DR(P, "w_dn", [D_FF, 1024], F32, "ExternalInput")
    vecC = DR(P, "vecC", [128, 192], F32, "ExternalInput")
    x2T = DR(P, "x2T", [1024, TOK], F32, "ExternalOutput")
    OA = DR(P, "OA", [NR_OA, TOK], BF16, "ExternalOutput")
    PS = alloc_psum(P)
    wbig = P.sb("wbig", [128, 8 * 2 * D_FF], BF16)
    d["wbig"] = wbig
    for k in range(8):
        P.dma("pool", wbig[:, k * 5632:(k + 1) * 5632], w_up[k * 128:(k + 1) * 128, :], reads=[w_up.r()], writes=[wbig], semkey="wbig")
    wdn = P.sb("wdn", [128, 22 * 1024], BF16)
    for j in range(22):
        P.dma("pool", wdn[:, j * 1024:(j + 1) * 1024], w_dn[j * 128:(j + 1) * 128, :], reads=[w_dn.r()], writes=[wdn], semkey="wdn")
    vec = P.sb("vecC", [128, 192], F32)
    P.dma("sp", vec[:, :], vecC[:, :], reads=[vecC.r()], writes=[vec])
    onesLN = P.sb("onesLN", [128, 128], F32)
    P.op("pool", lambda e: e.memset(onesLN[:, :], 1.0 / 1024.0), [], [onesLN])

    x1c = [P.sb("x1c%d" % k, [128, TF + 2], F32) for k in range(8)]
    x1b = [P.sb("x1b%d" % k, [128, TF + 2], BF16) for k in range(8)]
    gb = [P.sb("gb%d" % j, [128, TF], BF16) for j in range(22)]
    acc = [P.sb("acc%d" % i, [128, TF], F32) for i in range(4)]
    sg = [P.sb("sg%d" % i, [128, TF], F32) for i in range(2)]
    z = [P.sb("z%d" % k, [128, TF], F32) for k in range(8)]
    oc = [P.sb("oc%d" % k, [128, TF], F32) for k in range(2)]
    tmp = {"mean": P.sb("ln_mean", [128, TF], F32), "rstd": P.sb("ln_rstd", [128, TF], F32),
           "sq": [P.sb("ln_sq%d" % k, [128, TF], F32) for k in range(2)], "eps": make_eps(P, "epsln", LN_EPS)}
    banks = [PS["S0"], PS["S1"], PS["S2"], PS["ACC0"], PS["ACC1"], PS["M2"]]
    nbk = 0
    na = 0
    for t in range(TOK // TF):
        c0 = t * TF
        for k in range(8):
            P.dma("sp", x1c[k][:, :], x1h[k * 128:(k + 1) * 128, c0:c0 + TF + 2], reads=[x1h.r()], writes=[x1c[k]])
            P.op("pool" if k % 2 == 0 else "dve", lambda e, k=k: e.tensor_copy(out=x1b[k][:, :], in_=x1c[k][:, :]), [x1c[k]], [x1b[k]])
        for j in range(22):
            accs = []
            for which in range(2):
                ch = which * 22 + j
                col0 = ch * 128
                ps = banks[nbk % 6]
                nbk += 1
                for k in range(8):
                    P.op("pe", lambda e, k=k, ps=ps, col0=col0: e.matmul(ps[:, 0:TF], lhsT=wbig[:, k * 5632 + col0:k * 5632 + col0 + 128],
                                                                         rhs=x1b[k][:, 1:TF + 1], start=(k == 0), stop=(k == 7)),
                         [wbig, x1b[k]], [ps])
                for k in range(8):
                    P.op("pe", lambda e, k=k, ps=ps, col0=col0: e.matmul(ps[:, 384:385], lhsT=wbig[:, k * 5632 + col0:k * 5632 + col0 + 128],
                                                                         rhs=x1b[k][:, 0:1], start=(k == 0), stop=(k == 7)),
                         [wbig, x1b[k]], [ps])
                for k in range(8):
                    P.op("pe", lambda e, k=k, ps=ps, col0=col0: e.matmul(ps[:, 448:449], lhsT=wbig[:, k * 5632 + col0:k * 5632 + col0 + 128],
                                                                         rhs=x1b[k][:, TF + 1:TF + 2], start=(k == 0), stop=(k == 7)),
                         [wbig, x1b[k]], [ps])
                a = acc[na % 4]
                na += 1
                w0, w1, w2, bb = (vec[:, 16 + ch:17 + ch], vec[:, 60 + ch:61 + ch], vec[:, 104 + ch:105 + ch], vec[:, 148 + ch:149 + ch])
                P.op("act", lambda e, a=a, ps=ps, w1=w1, bb=bb: e.activation(out=a[:, :], in_=ps[:, 0:TF], func=AF.Identity, scale=w1, bias=bb),
                     [ps, vec], [a])
                P.op("dve", lambda e, a=a, ps=ps, w0=w0: e.scalar_tensor_tensor(out=a[:, 1:TF], in0=ps[:, 0:TF - 1], scalar=w0, in1=a[:, 1:TF],
                                                                                op0=ALU.mult, op1=ALU.add), [ps, a, vec], [a])
                P.op("dve", lambda e, a=a, ps=ps, w2=w2: e.scalar_tensor_tensor(out=a[:, 0:TF - 1], in0=ps[:, 1:TF], scalar=w2, in1=a[:, 0:TF - 1],
                                                                                op0=ALU.mult, op1=ALU.add), [ps, a, vec], [a])
                P.op("dve", lambda e, a=a, ps=ps, w0=w0: e.scalar_tensor_tensor(out=a[:, 0:1], in0=ps[:, 384:385], scalar=w0, in1=a[:, 0:1],
                                                                                op0=ALU.mult, op1=ALU.add), [ps, a, vec], [a])
                P.op("dve", lambda e, a=a, ps=ps, w2=w2: e.scalar_tensor_tensor(out=a[:, TF - 1:TF], in0=ps[:, 448:449], scalar=w2,
                                                                                in1=a[:, TF - 1:TF], op0=ALU.mult, op1=ALU.add),
                     [ps, a, vec], [a])
                accs.append(a)
            s = sg[j % 2]
            P.op("act", lambda e, s=s, a=accs[0]: e.activation(out=s[:, :], in_=a[:, :], func=AF.Silu), [accs[0]], [s])
            P.op("pool", lambda e, s=s, a=accs[1], j=j: e.tensor_tensor(out=gb[j][:, :], in0=s[:, :], in1=a[:, :], op=ALU.mult),
                 [s, accs[1]], [gb[j]])
        for o in range(8):
            ps = banks[nbk % 6]
            nbk += 1
            for j in range(22):
                P.op("pe", lambda e, j=j, o=o, ps=ps: e.matmul(ps[:, 0:TF], lhsT=wdn[:, j * 1024 + o * 128:j * 1024 + (o + 1) * 128],
                                                               rhs=gb[j][:, :], start=(j == 0), stop=(j == 21)), [wdn, gb[j]], [ps])
            P.op("dve", lambda e, o=o, ps=ps: e.scalar_tensor_tensor(out=z[o][:, :], in0=x1c[o][:, 1:TF + 1], scalar=ALPHA, in1=ps[:, 0:TF],
                                                                     op0=ALU.mult, op1=ALU.add), [x1c[o], ps], [z[o]])
        outs = [oc[o % 2] for o in range(8)]

        def store(o, c0=c0, t=t):
            P.dma("sp", x2T[o * 128:(o + 1) * 128, c0:c0 + TF], oc[o % 2][:, :], reads=[oc[o % 2]], writes=[x2T.r((o, t // 2))],
                  semkey=oc[o % 2].name)
        emit_layernorm(P, PS, z, onesLN, vec, 0, 8, tmp, outs, ncols=TF, store=store)

    P.barrier()
    ar = Arena([wdn.t[:, :], wbig.t[:, 8 * NC_EXT:8 * 2 * D_FF]])
    W = emit_phaseA_setup(P, d, alloc=ar.alloc)
    xf = [ar.alloc("xf%d" % i, [128, T], F32) for i in range(2)]
    xb = [[ar.alloc("xb%d_%d" % (s, k), [128, T], BF16) for k in range(8)] for s in range(1)] * 2
    for t in range(NT):
        for k in range(8):
            f = xf[k % 2]
            P.dma("sp", f[:, :], x2T[k * 128:(k + 1) * 128, t * T:(t + 1) * T], reads=[x2T.r((k, t))], writes=[f])
            P.op("dve" if k % 2 == 0 else "pool", lambda e, f=f, k=k, t=t: e.tensor_copy(out=xb[t % 2][k][:, :], in_=f[:, :]),
                 [f], [xb[t % 2][k]])
        emit_phaseA_tile(P, PS, W, d, t, xb[t % 2], OA)
    P.finish("sp", OA.allres() + x2T.allres())
    P.emit()
    return nc


_PROGS = {}


def _prog(name):
    if name not in _PROGS:
        _PROGS[name] = {"A": build_A, "BC1": build_BC1, "C2A": build_C2A}[name]()
    return _PROGS[name]


def _run(name, in_maps):
    res = run_bass_kernel_spmd(_prog(name), in_maps, core_ids=list(range(NCORE)))
    return res.results


def _rope_perm(ncols, group, rot):
    half = rot // 2
    idx = np.arange(ncols)
    j = idx % group
    out = idx.copy()
    out[j < half] += half
    out[(j >= half) & (j < rot)] -= half
    return out


def _w_ext(w_in_l):
    a, b, c = 768, 1536, 1792
    dq, dk, dv = w_in_l[:, 0:256], w_in_l[:, 256:512], w_in_l[:, 512:768]
    bq, bk, bv = w_in_l[:, 768:1024], w_in_l[:, 1024:1280], w_in_l[:, 1280:1536]
    u = w_in_l[:, 1536:1792]
    cq, ckv, kr = w_in_l[:, 1792:2048], w_in_l[:, 2048:2176], w_in_l[:, 2176:2208]
    pd = _rope_perm(256, 32, 8)
    pb = _rope_perm(256, 64, 16)
    pk = _rope_perm(32, 32, 32)
    pad = np.zeros((1024, 64), np.float32)
    return np.ascontiguousarray(np.concatenate(
        [dq, dk, dq[:, pd], dk[:, pd], dv, bq, bk, bq[:, pb], bk[:, pb], bv, u, cq, ckv, kr, kr[:, pk], pad], axis=1))


def _wuq_ext(w_uq_l):
    cols = []
    pm = _rope_perm(32, 32, 32)
    for h in range(4):
        blk = w_uq_l[:, 96 * h:96 * h + 96]
        sw = np.concatenate([blk[:, 0:64], blk[:, 64:96][:, pm]], axis=1)
        cols += [blk, sw]
    return np.ascontiguousarray(np.concatenate(cols, axis=1))


def _rope_rows(n, group, rot, offset=0):
    half = rot // 2
    f = np.zeros(128, np.float32)
    s = np.zeros(128, np.float32)
    inv = (THETA ** (-(np.arange(half, dtype=np.float32) * 2.0 / rot))).astype(np.float32)
    for r in range(n):
        j = r % group - offset
        if 0 <= j < half:
            f[r], s[r] = inv[j], -1.0
        elif half <= j < rot:
            f[r], s[r] = inv[j - half], 1.0
    return f, s


def _vecA(q_norm_l, kv_norm_l):
    v = np.zeros((128, 16), np.float32)
    for i, (n, group, rot, off) in enumerate(((128, 32, 8, 0), (128, 64, 16, 0), (96, 96, 32, 64), (32, 32, 32, 0))):
        v[:, 2 * i], v[:, 2 * i + 1] = _rope_rows(n, group, rot, off)
    v[:, 8] = q_norm_l[0:128]
    v[:, 9] = q_norm_l[128:256]
    v[:, 10] = kv_norm_l
    return v


def _phaseA_inputs(inp, l, c):
    b, i = divmod(c, 4)
    return {"w_ext": _w_ext(inp["w_in"][l]), "wuq_ext": _wuq_ext(inp["mla_w_uq"][l]),
            "w_ukv": np.ascontiguousarray(inp["mla_w_ukv"][l]), "vecA": _vecA(inp["mla_q_norm"][l], inp["mla_kv_norm"][l]),
            "pos": np.ascontiguousarray(inp["positions"][b, i * TOK:(i + 1) * TOK].reshape(1, TOK))}


def _vext(vT):
    n = vT.shape[1]
    e = np.ones((n, 128), NPBF)
    e[:, 0:64] = vT.T
    return e


def _blocked(e):
    n = e.shape[0]
    return np.ascontiguousarray(e.reshape(n // 128, 128, e.shape[1]).transpose(1, 0, 2).reshape(128, -1))


_CONST = {}


def _consts():
    if _CONST:
        return _CONST
    dd = np.arange(-1408, 2944 - 1408 + 0)
    p = np.arange(128)[:, None]
    c = np.arange(2944)[None, :]
    dlt = p - c + 1408
    ad = np.abs(dlt)
    m = (ad <= 64).astype(np.float32) + ((dlt % 4 == 0) & (ad <= 256)) + ((dlt % 16 == 0) & (ad <= 1024))
    _CONST["dstrip"] = m.astype(NPBF)
    c = np.arange(1152)[None, :]
    dlt = p - c + 512
    ps = []
    for w in (2, 4, 8, 16):
        ps.append(((dlt >= -(w // 2)) & (dlt <= w // 2 - 1)).astype(np.float32))
    _CONST["pstrip"] = np.ascontiguousarray(np.concatenate(ps, axis=1).astype(NPBF))
    tt = np.arange(SEQ)
    ic = np.zeros((4, SEQ), np.float32)
    for g, w in enumerate((2, 4, 8, 16)):
        lo = np.clip(tt - w // 2, 0, SEQ - 1)
        hi = np.clip(tt + w - w // 2 - 1, 0, SEQ - 1)
        ic[g] = 1.0 / (hi - lo + 1).astype(np.float32)
    _CONST["invc"] = ic
    return _CONST


def _bc1_inputs(inp, l, OAs, xTs):
    cst = _consts()
    maps = []
    lam_init = 0.8 - 0.6 * math.exp(-0.3 * l)
    vecB = np.zeros((128, 160), np.float32)
    vecB[:, 0:8] = inp["ln1_g"][l].reshape(8, 128).T
    vecB[:, 8:16] = inp["ln1_b"][l].reshape(8, 128).T
    vecB[0:64, 16:20] = inp["pool_scale"][l].reshape(4, 64).T
    vecB[0:64, 20] = inp["diff_subln"][l]
    vecB[:, 21] = lam_init
    vecB[:, 22] = 1.0 - lam_init
    vecB[:, 32:160] = inp["diff_lambda"][l].reshape(1, 128)
    pool_w = np.ascontiguousarray(inp["pool_w"][l].transpose(1, 0, 2).reshape(64, 256))
    w_out = np.ascontiguousarray(inp["w_out"][l])
    for b in range(2):
        OAb = np.concatenate([OAs[4 * b + i] for i in range(4)], axis=1)
        dKT = np.ascontiguousarray(OAb[R_DK:R_DK + 256])
        dVx = np.stack([_blocked(_vext(OAb[R_DV + 64 * h:R_DV + 64 * h + 64])) for h in range(4)])
        mKT = np.stack([np.concatenate([OAb[R_MKV + 128 * h:R_MKV + 128 * h + 64], OAb[R_MKR:R_MKR + 32]], axis=0) for h in range(4)])
        mVx = np.stack([_blocked(_vext(OAb[R_MKV + 128 * h + 64:R_MKV + 128 * h + 128])) for h in range(4)])
        bKp = np.zeros((256, SEQ + 2048), NPBF)
        bKp[:, 1024:1024 + SEQ] = OAb[R_BK:R_BK + 256]
        bVp = []
        for h in range(4):
            e = np.zeros((SEQ + 2048, 128), NPBF)
            e[1024:1024 + SEQ] = _vext(OAb[R_BV + 64 * h:R_BV + 64 * h + 64])
            bVp.append(e)
        utp = np.zeros((SEQ + 256, 256), NPBF)
        utp[128:128 + SEQ] = OAb[R_U:R_U + 256].T
        for i in range(4):
            c = 4 * b + i
            OAc = OAs[c]
            m = {
                "QT": np.ascontiguousarray(np.concatenate([OAc[R_DQ:R_DQ + 256], OAc[R_BQ:R_BQ + 256], OAc[R_MQ:R_MQ + 384]], axis=0)),
                "dKT": dKT, "dVx": dVx, "mKT": np.ascontiguousarray(mKT), "mVx": mVx,
                "bKT": np.ascontiguousarray(bKp[:, TOK * i:TOK * i + 6144]),
                "bVx": np.stack([_blocked(bVp[h][TOK * i:TOK * i + 6144]) for h in range(4)]),
                "uTok": _blocked(utp[TOK * i:TOK * i + 34 * 128]),
                "u16T": np.ascontiguousarray(OAc[R_U:R_U + 256]),
                "invc": np.ascontiguousarray(np.broadcast_to(cst["invc"][:, None, TOK * i:TOK * (i + 1)], (4, 64, TOK))),
                "dstrip": cst["dstrip"], "pstrip": cst["pstrip"],
                "xT": xTs[c], "w_out": w_out, "pool_w": pool_w, "vecB": vecB,
            }
            maps.append(m)
    return maps


def _c2a_inputs(inp, l, lnext, x1Ts):
    vecC = np.zeros((128, 192), np.float32)
    vecC[:, 0:8] = inp["ln2_g"][l].reshape(8, 128).T
    vecC[:, 8:16] = inp["ln2_b"][l].reshape(8, 128).T
    for j in range(3):
        vecC[:, 16 + 44 * j:16 + 44 * (j + 1)] = inp["ffn_conv_w"][l][j].reshape(44, 128).T
    vecC[:, 148:192] = inp["ffn_conv_b"][l].reshape(44, 128).T
    w_up = np.ascontiguousarray(inp["ffn_w_up"][l])
    w_dn = np.ascontiguousarray(inp["ffn_w_down"][l])
    maps = []
    for b in range(2):
        xb = np.zeros((1024, SEQ + 2), np.float32)
        xb[:, 1:1 + SEQ] = np.concatenate([x1Ts[4 * b + i] for i in range(4)], axis=1)
        for i in range(4):
            c = 4 * b + i
            m = _phaseA_inputs(inp, lnext, c)
            m.update({"x1h": np.ascontiguousarray(xb[:, TOK * i:TOK * i + TOK + 2]), "w_up": w_up, "w_dn": w_dn, "vecC": vecC})
            maps.append(m)
    return maps


def kernel(**inp):
    inp = {k: np.asarray(v) for k, v in inp.items()}
    x = inp["x"]
    xTs = [np.ascontiguousarray(x[c // 4, (c % 4) * TOK:(c % 4 + 1) * TOK, :].T) for c in range(NCORE)]
    maps = []
    for c in range(NCORE):
        m = _phaseA_inputs(inp, 0, c)
        m["xT"] = xTs[c]
        maps.append(m)
    r = _run("A", maps)
    OAs = [np.asarray(r[c]["OA"]) for c in range(NCORE)]
    for l in range(DEPTH):
        r = _run("BC1", _bc1_inputs(inp, l, OAs, xTs))
        x1Ts = [np.asarray(r[c]["x1T"]) for c in range(NCORE)]
        r = _run("C2A", _c2a_inputs(inp, l, (l + 1) % DEPTH, x1Ts))
        xTs = [np.asarray(r[c]["x2T"]) for c in range(NCORE)]
        OAs = [np.asarray(r[c]["OA"]) for c in range(NCORE)]
    out = np.zeros((2, SEQ, D_MODEL), np.float32)
    for c in range(NCORE):
        out[c // 4, (c % 4) * TOK:(c % 4 + 1) * TOK, :] = xTs[c].T
    return out
```

```python
import math
import numpy as np
import ml_dtypes
import concourse.bass as bass
import concourse.mybir as mybir
from concourse.bass_utils import run_bass_kernel_spmd

F32 = mybir.dt.float32
BF16 = mybir.dt.bfloat16
I32 = mybir.dt.int32
ALU = mybir.AluOpType
AF = mybir.ActivationFunctionType
NPBF = ml_dtypes.bfloat16

D_MODEL = 1024
SEQ = 16384
DEPTH = 4
NCORE = 8
TOK = 4096
T = 512
NT = TOK // T
D_FF = 2816
ALPHA = (2 * DEPTH) ** 0.25
LN_EPS = 1e-5
RMS_EPS = 1e-6
THETA = 500000.0
TWO_PI = 2.0 * math.pi
CW1 = 6.28125
CW2 = TWO_PI - 6.28125
PI_LO = 3.1415925

R_DQ, R_DK, R_DV, R_BQ, R_BK, R_BV, R_U, R_MQ, R_MKV, R_MKR = 0, 256, 512, 768, 1024, 1280, 1536, 1792, 2176, 2688
NR_OA = 2720
NC_EXT = 26 * 128

COMPUTE = ("pe", "act", "dve", "pool")


class Res:
    __slots__ = ("name", "t", "last_write", "reads")

    def __init__(self, name, t=None):
        self.name = name
        self.t = t
        self.last_write = None
        self.reads = []

    def __getitem__(self, idx):
        return self.t[idx]


class Prog:
    def __init__(self, nc):
        self.nc = nc
        self.ops = {e: [] for e in COMPUTE + ("sp",)}
        self.cnt = {e: 0 for e in COMPUTE}
        self.esem = {e: nc.alloc_semaphore("s_" + e) for e in COMPUTE}
        self.dsems = {}
        self.waited = {e: {} for e in COMPUTE + ("sp",)}
        self.nops = 0

    def sb(self, name, shape, dt):
        return Res(name, self.nc.alloc_sbuf_tensor("sb_" + name, list(shape), dt))

    def ps(self, name, shape, dt=F32):
        return Res(name, self.nc.alloc_psum_tensor("ps_" + name, list(shape), dt))

    def dram(self, name, shape, dt, kind="Internal"):
        return Res(name, self.nc.dram_tensor(name, list(shape), dt, kind=kind))

    def _dsem(self, key):
        if key not in self.dsems:
            self.dsems[key] = [self.nc.alloc_semaphore("d_%d" % len(self.dsems)), 0, 0]
        return self.dsems[key]

    def _collect(self, eng, reads, writes):
        deps = []
        for r in reads:
            if r.last_write is not None:
                deps.append(r.last_write)
        for w in writes:
            if w.last_write is not None:
                deps.append(w.last_write)
            deps.extend(w.reads)
        waits = {}
        for kind, key, val, teng in deps:
            if kind == "c":
                if teng == "pe" and eng == "pe":
                    continue
                waits[("c", key)] = max(waits.get(("c", key), 0), val)
            else:
                st = self.dsems[key]
                st[2] = st[1]
                waits[("d", key)] = max(waits.get(("d", key), 0), st[1])
        out = []
        wd = self.waited[eng]
        for k, v in waits.items():
            if wd.get(k, 0) >= v:
                continue
            wd[k] = v
            out.append((self.esem[k[1]] if k[0] == "c" else self.dsems[k[1]][0], v))
        return out

    def _mark(self, tok, reads, writes):
        for r in reads:
            r.reads.append(tok)
        for w in writes:
            w.last_write = tok
            w.reads = []

    def op(self, eng, fn, reads=(), writes=()):
        reads, writes = list(reads), list(writes)
        waits = self._collect(eng, reads, writes)
        self.cnt[eng] += 1
        tok = ("c", eng, self.cnt[eng], eng)
        self.ops[eng].append((waits, fn, (self.esem[eng], 1)))
        self._mark(tok, reads, writes)
        self.nops += 1 + len(waits)

    def dma(self, q, out_ap, in_ap, reads=(), writes=(), semkey=None):
        reads, writes = list(reads), list(writes)
        if semkey is None:
            semkey = writes[0].name if (writes and writes[0].t is not None and not writes[0].name.startswith("D:")) else (
                reads[0].name if reads else writes[0].name)
        st = self._dsem(semkey)
        waits = self._collect(q, reads, writes)
        if st[2] > 0 and self.waited[q].get(("d", semkey), 0) < st[2]:
            waits.append((st[0], st[2]))
            self.waited[q][("d", semkey)] = st[2]
        st[1] += 16
        tok = ("d", semkey, st[1], q)

        def fn(e, out_ap=out_ap, in_ap=in_ap):
            return e.dma_start(out=out_ap, in_=in_ap)
        self.ops[q].append((waits, fn, (st[0], 16)))
        self._mark(tok, reads, writes)
        self.nops += 1 + len(waits)

    def barrier(self):
        allw = [(("c", e), self.cnt[e]) for e in COMPUTE if self.cnt[e] > 0]
        for key, st in self.dsems.items():
            if st[1] > 0:
                st[2] = st[1]
                allw.append((("d", key), st[1]))
        for eng in COMPUTE + ("sp",):
            waits = []
            wd = self.waited[eng]
            for k, v in allw:
                if wd.get(k, 0) >= v:
                    continue
                wd[k] = v
                waits.append((self.esem[k[1]] if k[0] == "c" else self.dsems[k[1]][0], v))
            self.ops[eng].append((waits, None, None))
            self.nops += len(waits)

    def finish(self, eng, resources):
        waits = self._collect(eng, list(resources), [])
        self.ops[eng].append((waits, None, None))

    def emit(self):
        with self.nc.Block() as block:
            def run(name):
                def body(e):
                    for waits, fn, inc in self.ops[name]:
                        for sem, v in waits:
                            e.wait_ge(sem, v)
                        if fn is not None:
                            fn(e).then_inc(inc[0], inc[1])
                return body
            block.tensor(run("pe"))
            block.scalar(run("act"))
            block.vector(run("dve"))
            block.gpsimd(run("pool"))
            block.sync(run("sp"))


class Arena:
    def __init__(self, tensors):
        self.chunks = [[t, 0, t.shape[1] * 2] for t in tensors]

    def alloc(self, name, shape, dt):
        nbytes = shape[1] * (4 if dt in (F32, I32) else 2)
        nbytes = (nbytes + 63) // 64 * 64
        for ch in self.chunks:
            if ch[1] + nbytes <= ch[2]:
                off = ch[1]
                ch[1] += nbytes
                v = ch[0][0:shape[0], off // 2:(off + nbytes) // 2]
                if dt != BF16:
                    v = v.bitcast(dt)
                return Res(name, v[:, 0:shape[1]])
        raise RuntimeError("arena full: " + name)


class DR:
    def __init__(self, P, name, shape, dt, kind="Internal"):
        self.t = P.nc.dram_tensor(name, list(shape), dt, kind=kind)
        self.name = name
        self.regs = {}
        self.whole = Res("D:" + name)

    def r(self, key=None):
        if key is None:
            return self.whole
        if key not in self.regs:
            self.regs[key] = Res("D:%s:%s" % (self.name, str(key)))
        return self.regs[key]

    def __getitem__(self, idx):
        return self.t[idx]

    def allres(self):
        return [self.whole] + list(self.regs.values())


def bcast_rows(dr, col0, ncols, nparts=128):
    return bass.AP(dr.t, col0, [[0, nparts], [1, ncols]])


def alloc_psum(P):
    return {n: P.ps(n, [128, 512]) for n in ("S0", "S1", "S2", "ACC0", "ACC1", "M0", "M1", "M2")}


def emit_rstd(P, out, src, eps_t, m, ncols):
    P.op("act", lambda e: e.activation(out=out[0:m, 0:ncols], in_=src[0:m, 0:ncols], func=AF.Sqrt, bias=eps_t[0:m, 0:1], scale=1.0),
         [src, eps_t], [out])
    P.op("dve", lambda e: e.reciprocal(out=out[0:m, 0:ncols], in_=out[0:m, 0:ncols]), [out], [out])


def make_eps(P, name, val, alloc=None):
    t = (alloc or P.sb)(name, [128, 1], F32)
    P.op("pool", lambda e: e.memset(t[:, :], val), [], [t])
    return t


def emit_layernorm(P, PS, z, onesLN, vec, gcol, bcol, tmp, outs, ncols=T, store=None):
    mps, vps = PS["M0"], PS["M1"]
    mean_sb, rstd_sb, sq = tmp["mean"], tmp["rstd"], tmp["sq"]
    for o in range(8):
        P.op("pe", lambda e, o=o: e.matmul(mps[:, 0:ncols], lhsT=onesLN[:, :], rhs=z[o][:, 0:ncols],
                                           start=(o == 0), stop=(o == 7)), [onesLN, z[o]], [mps])
    P.op("act", lambda e: e.activation(out=mean_sb[:, 0:ncols], in_=mps[:, 0:ncols], func=AF.Copy), [mps], [mean_sb])
    for o in range(8):
        eng = "dve" if o % 2 == 0 else "pool"
        P.op(eng, lambda e, o=o: e.tensor_tensor(out=z[o][:, 0:ncols], in0=z[o][:, 0:ncols], in1=mean_sb[:, 0:ncols],
                                                 op=ALU.subtract), [z[o], mean_sb], [z[o]])
        s = sq[o % 2]
        P.op("act", lambda e, o=o, s=s: e.activation(out=s[:, 0:ncols], in_=z[o][:, 0:ncols], func=AF.Square), [z[o]], [s])
        P.op("pe", lambda e, o=o, s=s: e.matmul(vps[:, 0:ncols], lhsT=onesLN[:, :], rhs=s[:, 0:ncols],
                                                start=(o == 0), stop=(o == 7)), [onesLN, s], [vps])
    emit_rstd(P, rstd_sb, vps, tmp["eps"], 128, ncols)
    for o in range(8):
        eng = "dve" if o % 2 == 1 else "pool"
        P.op(eng, lambda e, o=o: e.tensor_tensor(out=z[o][:, 0:ncols], in0=z[o][:, 0:ncols], in1=rstd_sb[:, 0:ncols],
                                                 op=ALU.mult), [z[o], rstd_sb], [z[o]])
        P.op("act", lambda e, o=o: e.activation(out=outs[o][:, 0:ncols], in_=z[o][:, 0:ncols], func=AF.Identity,
                                                scale=vec[:, gcol + o:gcol + o + 1], bias=vec[:, bcol + o:bcol + o + 1]),
             [z[o], vec], [outs[o]])
        if store is not None:
            store(o)


def emit_phaseA_setup(P, d, alloc=None):
    W = {}
    if alloc is None:
        alloc = P.sb
    class _A:
        sb = staticmethod(alloc)
    PA = _A
    W["wA"] = d["wbig"]
    wA = W["wA"]
    for k in range(8):
        P.dma("pool", wA.t[:, k * NC_EXT:(k + 1) * NC_EXT], d["w_ext"][k * 128:(k + 1) * 128, :],
              reads=[d["w_ext"].r()], writes=[wA], semkey="wA")
    W["wuq"] = PA.sb("wuq", [128, 2 * 768], BF16)
    for k in range(2):
        P.dma("pool", W["wuq"][:, k * 768:(k + 1) * 768], d["wuq_ext"][k * 128:(k + 1) * 128, :], reads=[d["wuq_ext"].r()],
              writes=[W["wuq"]], semkey="wuq")
    W["wukv"] = PA.sb("wukv", [128, 512], BF16)
    P.dma("pool", W["wukv"][:, :], d["w_ukv"][:, :], reads=[d["w_ukv"].r()], writes=[W["wukv"]], semkey="wukv")
    W["vecA"] = PA.sb("vecA", [128, 16], F32)
    P.dma("sp", W["vecA"][:, :], d["vecA"][:, :], reads=[d["vecA"].r()], writes=[W["vecA"]])
    W["onesq"] = PA.sb("onesq", [128, 128], F32)
    P.op("pool", lambda e: e.memset(W["onesq"][:, :], 1.0 / 256.0), [], [W["onesq"]])
    W["oneskv"] = PA.sb("oneskv", [128, 128], F32)
    P.op("pool", lambda e: e.memset(W["oneskv"][:, :], 1.0 / 128.0), [], [W["oneskv"]])
    W["posi"] = PA.sb("posi", [128, T], I32)
    W["posf"] = PA.sb("posf", [128, T], F32)
    W["ang"] = PA.sb("ang", [128, T], F32)
    W["r1"] = PA.sb("r1", [128, T], F32)
    W["r2"] = PA.sb("r2", [128, T], F32)
    W["kf"] = PA.sb("kf", [128, T], F32)
    W["ki"] = PA.sb("ki", [128, T], I32)
    W["C"] = [PA.sb("ropeC%d" % i, [128, T], F32) for i in range(4)]
    W["S"] = [PA.sb("ropeS%d" % i, [128, T], F32) for i in range(4)]
    W["t1"] = [PA.sb("ra_t1_%d" % i, [128, T], F32) for i in range(2)]
    W["t2"] = [PA.sb("ra_t2_%d" % i, [128, T], F32) for i in range(2)]
    W["ob"] = [PA.sb("ra_ob_%d" % i, [128, T], BF16) for i in range(3)]
    W["cq"] = [PA.sb("cq%d" % i, [128, T], F32) for i in range(3)]
    W["cqsq"] = [PA.sb("cqsq%d" % i, [128, T], F32) for i in range(2)]
    W["rs"] = PA.sb("rs_rstd", [128, T], F32)
    W["cqn"] = [PA.sb("cqn%d" % i, [128, T], BF16) for i in range(3)]
    W["epsr"] = make_eps(P, "epsr_A", RMS_EPS, alloc)
    W["pibias"] = PA.sb("pibias", [128, 1], F32)
    P.op("pool", lambda e: e.memset(W["pibias"][:, :], PI_LO), [], [W["pibias"]])
    return W


def emit_phaseA_tile(P, PS, W, d, t, xb, OA):
    wA, vecA = W["wA"], W["vecA"]
    c0 = t * T
    banks = [PS["S0"], PS["S1"], PS["S2"], PS["ACC0"], PS["ACC1"], PS["M2"]]
    st = {"b": 0, "ob": 0, "t": 0}

    def nb():
        b = banks[st["b"] % len(banks)]
        st["b"] += 1
        return b

    def nob():
        b = W["ob"][st["ob"] % 3]
        st["ob"] += 1
        return b

    def proj(ps, col0, m, rows=128):
        for k in range(8):
            P.op("pe", lambda e, k=k: e.matmul(ps[0:m, :], lhsT=wA.t[0:128, k * NC_EXT + col0:k * NC_EXT + col0 + m],
                                               rhs=xb[k][:, :], start=(k == 0), stop=(k == 7)), [wA, xb[k]], [ps])

    def store(ob, m, row0):
        P.dma("sp", OA[row0:row0 + m, c0:c0 + T], ob[0:m, :], reads=[ob], writes=[OA.r((row0, t))], semkey=ob.name)

    P.dma("sp", W["posi"][:, :], bcast_rows(d["pos"], c0, T), reads=[d["pos"].r()], writes=[W["posi"]])
    P.op("dve", lambda e: e.tensor_copy(out=W["posf"][:, :], in_=W["posi"][:, :]), [W["posi"]], [W["posf"]])
    for i in range(4):
        m = (128, 128, 96, 32)[i]
        ang, r1, r2, kf, ki = W["ang"], W["r1"], W["r2"], W["kf"], W["ki"]
        P.op("dve", lambda e, i=i, m=m: e.tensor_scalar(out=ang[0:m, :], in0=W["posf"][0:m, :],
                                                        scalar1=vecA[0:m, 2 * i:2 * i + 1], scalar2=None, op0=ALU.mult),
             [W["posf"], vecA], [ang])
        P.op("dve", lambda e, m=m: e.tensor_scalar(out=kf[0:m, :], in0=ang[0:m, :], scalar1=1.0 / TWO_PI, scalar2=None, op0=ALU.mult),
             [ang], [kf])
        P.op("dve", lambda e, m=m: e.tensor_copy(out=ki[0:m, :], in_=kf[0:m, :]), [kf], [ki])
        P.op("dve", lambda e, m=m: e.tensor_copy(out=kf[0:m, :], in_=ki[0:m, :]), [ki], [kf])
        P.op("dve", lambda e, m=m: e.scalar_tensor_tensor(out=r1[0:m, :], in0=kf[0:m, :], scalar=-CW1, in1=ang[0:m, :],
                                                          op0=ALU.mult, op1=ALU.add), [kf, ang], [r1])
        P.op("dve", lambda e, m=m: e.scalar_tensor_tensor(out=r1[0:m, :], in0=kf[0:m, :], scalar=-CW2, in1=r1[0:m, :],
                                                          op0=ALU.mult, op1=ALU.add), [kf, r1], [r1])
        P.op("dve", lambda e, m=m: e.tensor_scalar(out=r2[0:m, :], in0=r1[0:m, :], scalar1=0.5 * math.pi, scalar2=None, op0=ALU.add),
             [r1], [r2])
        for rr in (r1, r2):
            for thr, cmp_, per in ((PI_LO, ALU.is_gt, -TWO_PI), (-PI_LO, ALU.is_lt, TWO_PI)):
                P.op("dve", lambda e, m=m, rr=rr, thr=thr, cmp_=cmp_, per=per: e.tensor_scalar(
                    out=ang[0:m, :], in0=rr[0:m, :], scalar1=thr, scalar2=per, op0=cmp_, op1=ALU.mult), [rr], [ang])
                P.op("dve", lambda e, m=m, rr=rr: e.tensor_tensor(out=rr[0:m, :], in0=rr[0:m, :], in1=ang[0:m, :], op=ALU.add),
                     [rr, ang], [rr])
            P.op("dve", lambda e, m=m, rr=rr: e.tensor_scalar(out=rr[0:m, :], in0=rr[0:m, :], scalar1=-PI_LO, scalar2=PI_LO,
                                                              op0=ALU.max, op1=ALU.min), [rr], [rr])
        P.op("act", lambda e, i=i, m=m: e.activation(out=W["S"][i][0:m, :], in_=r1[0:m, :], func=AF.Sin), [r1], [W["S"][i]])
        P.op("act", lambda e, i=i, m=m: e.activation(out=W["C"][i][0:m, :], in_=r2[0:m, :], func=AF.Sin), [r2], [W["C"][i]])
        P.op("dve", lambda e, i=i, m=m: e.tensor_scalar(out=W["S"][i][0:m, :], in0=W["S"][i][0:m, :],
                                                        scalar1=vecA[0:m, 2 * i + 1:2 * i + 2], scalar2=None, op0=ALU.mult),
             [W["S"][i], vecA], [W["S"][i]])

    def rope_out(pa, pb, m, pat, row0):
        i = st["t"] % 2
        st["t"] += 1
        t1, t2 = W["t1"][i], W["t2"][i]
        ob = nob()
        P.op("dve", lambda e: e.tensor_tensor(out=t1[0:m, :], in0=pa[0:m, :], in1=W["C"][pat][0:m, :], op=ALU.mult),
             [pa, W["C"][pat]], [t1])
        P.op("dve", lambda e: e.tensor_tensor(out=t2[0:m, :], in0=pb[0:m, :], in1=W["S"][pat][0:m, :], op=ALU.mult),
             [pb, W["S"][pat]], [t2])
        P.op("pool", lambda e: e.tensor_tensor(out=ob[0:m, :], in0=t1[0:m, :], in1=t2[0:m, :], op=ALU.add), [t1, t2], [ob])
        store(ob, m, row0)

    def plain_out(pa, m, row0):
        ob = nob()
        P.op("act", lambda e: e.activation(out=ob[0:m, :], in_=pa[0:m, :], func=AF.Copy), [pa], [ob])
        store(ob, m, row0)

    for j, row0 in enumerate((R_DQ, R_DQ + 128, R_DK, R_DK + 128)):
        pa, pb = nb(), nb()
        proj(pa, j * 128, 128)
        proj(pb, (4 + j) * 128, 128)
        rope_out(pa, pb, 128, 0, row0)
    for j, row0 in enumerate((R_DV, R_DV + 128)):
        pa = nb()
        proj(pa, (8 + j) * 128, 128)
        plain_out(pa, 128, row0)
    for j, row0 in enumerate((R_BQ, R_BQ + 128, R_BK, R_BK + 128)):
        pa, pb = nb(), nb()
        proj(pa, (10 + j) * 128, 128)
        proj(pb, (14 + j) * 128, 128)
        rope_out(pa, pb, 128, 1, row0)
    for j, row0 in enumerate((R_BV, R_BV + 128, R_U, R_U + 128)):
        pa = nb()
        proj(pa, (18 + j) * 128, 128)
        plain_out(pa, 128, row0)
    pa, pb = nb(), nb()
    proj(pa, 25 * 128, 32)
    proj(pb, 25 * 128 + 32, 32)
    rope_out(pa, pb, 32, 3, R_MKR)
    for j in range(3):
        pa = nb()
        proj(pa, (22 + j) * 128, 128)
        P.op("act", lambda e, j=j, pa=pa: e.activation(out=W["cq"][j][:, :], in_=pa[:, :], func=AF.Copy), [pa], [W["cq"][j]])
    for grp, (chunks, ones, eps_col) in enumerate((((0, 1), W["onesq"], 8), ((2,), W["oneskv"], 10))):
        msp = nb()
        for n, j in enumerate(chunks):
            s = W["cqsq"][n % 2]
            P.op("act", lambda e, j=j, s=s: e.activation(out=s[:, :], in_=W["cq"][j][:, :], func=AF.Square), [W["cq"][j]], [s])
            P.op("pe", lambda e, s=s, n=n, ones=ones, L=len(chunks), msp=msp: e.matmul(msp[:, :], lhsT=ones[:, :], rhs=s[:, :],
                                                                             start=(n == 0), stop=(n == L - 1)),
                 [ones, s], [msp])
        emit_rstd(P, W["rs"], msp, W["epsr"], 128, T)
        for j in chunks:
            P.op("dve", lambda e, j=j: e.tensor_tensor(out=W["cq"][j][:, :], in0=W["cq"][j][:, :], in1=W["rs"][:, :],
                                                       op=ALU.mult), [W["cq"][j], W["rs"]], [W["cq"][j]])
            P.op("dve", lambda e, j=j: e.tensor_scalar(out=W["cqn"][j][:, :], in0=W["cq"][j][:, :],
                                                       scalar1=vecA[:, 8 + j:9 + j], scalar2=None, op0=ALU.mult),
                 [W["cq"][j], vecA], [W["cqn"][j]])
    for h in range(4):
        pa, pb = nb(), nb()
        for which, ps in ((0, pa), (1, pb)):
            col0 = h * 192 + which * 96
            for k in range(2):
                P.op("pe", lambda e, k=k, ps=ps, col0=col0: e.matmul(ps[0:96, :], lhsT=W["wuq"][:, k * 768 + col0:k * 768 + col0 + 96],
                                                                     rhs=W["cqn"][k][:, :], start=(k == 0), stop=(k == 1)),
                     [W["wuq"], W["cqn"][k]], [ps])
        rope_out(pa, pb, 96, 2, R_MQ + 96 * h)
    for h in range(4):
        pa = nb()
        P.op("pe", lambda e, h=h, pa=pa: e.matmul(pa[:, :], lhsT=W["wukv"][:, h * 128:(h + 1) * 128], rhs=W["cqn"][2][:, :],
                                                  start=True, stop=True), [W["wukv"], W["cqn"][2]], [pa])
        plain_out(pa, 128, R_MKV + 128 * h)


def declare_phaseA_inputs(P):
    d = {}
    d["w_ext"] = DR(P, "w_ext", [1024, NC_EXT], F32, "ExternalInput")
    d["wuq_ext"] = DR(P, "wuq_ext", [256, 768], F32, "ExternalInput")
    d["w_ukv"] = DR(P, "w_ukv", [128, 512], F32, "ExternalInput")
    d["vecA"] = DR(P, "vecA", [128, 16], F32, "ExternalInput")
    d["pos"] = DR(P, "pos", [1, TOK], I32, "ExternalInput")
    return d


def build_A():
    nc = bass.Bass("TRN2", target_bir_lowering=False)
    P = Prog(nc)
    d = declare_phaseA_inputs(P)
    xT = DR(P, "xT", [1024, TOK], F32, "ExternalInput")
    OA = DR(P, "OA", [NR_OA, TOK], BF16, "ExternalOutput")
    PS = alloc_psum(P)
    d["wbig"] = P.sb("wbig", [128, 8 * NC_EXT], BF16)
    W = emit_phaseA_setup(P, d)
    xf = [P.sb("xf%d" % i, [128, T], F32) for i in range(2)]
    xb = [[P.sb("xb%d_%d" % (s, k), [128, T], BF16) for k in range(8)] for s in range(2)]
    for t in range(NT):
        for k in range(8):
            f = xf[k % 2]
            P.dma("sp", f[:, :], xT[k * 128:(k + 1) * 128, t * T:(t + 1) * T], reads=[xT.r()], writes=[f])
            P.op("dve" if k % 2 == 0 else "pool", lambda e, f=f, k=k, t=t: e.tensor_copy(out=xb[t % 2][k][:, :], in_=f[:, :]),
                 [f], [xb[t % 2][k]])
        emit_phaseA_tile(P, PS, W, d, t, xb[t % 2], OA)
    P.finish("sp", OA.allres())
    P.emit()
    return nc


def build_BC1():
    nc = bass.Bass("TRN2", target_bir_lowering=False)
    P = Prog(nc)
    QT = DR(P, "QT", [896, TOK], BF16, "ExternalInput")
    dKT = DR(P, "dKT", [256, SEQ], BF16, "ExternalInput")
    dVx = DR(P, "dVx", [4, 128, SEQ], BF16, "ExternalInput")
    bKT = DR(P, "bKT", [256, 6144], BF16, "ExternalInput")
    bVx = DR(P, "bVx", [4, 128, 6144], BF16, "ExternalInput")
    mKT = DR(P, "mKT", [4, 96, SEQ], BF16, "ExternalInput")
    mVx = DR(P, "mVx", [4, 128, SEQ], BF16, "ExternalInput")
    uTok = DR(P, "uTok", [128, 34 * 256], BF16, "ExternalInput")
    u16T = DR(P, "u16T", [256, TOK], BF16, "ExternalInput")
    invc = DR(P, "invc", [4, 64, TOK], F32, "ExternalInput")
    dstrip = DR(P, "dstrip", [128, 2944], BF16, "ExternalInput")
    pstrip = DR(P, "pstrip", [128, 4 * 1152], BF16, "ExternalInput")
    xT = DR(P, "xT", [1024, TOK], F32, "ExternalInput")
    w_out = DR(P, "w_out", [1024, 1024], F32, "ExternalInput")
    pool_w = DR(P, "pool_w", [64, 256], F32, "ExternalInput")
    vecB = DR(P, "vecB", [128, 160], F32, "ExternalInput")
    x1T = DR(P, "x1T", [1024, TOK], F32, "ExternalOutput")
    mixT = DR(P, "mixT", [1024, TOK], BF16)
    SU = [P.ps("SU%d" % i, [128, 1024]) for i in range(3)]
    PS = {"ACC0": P.ps("ACC0", [128, 512]), "ACC1": P.ps("ACC1", [128, 512]),
          "S0": SU[0], "S1": SU[1], "S2": SU[2], "M0": SU[0], "M1": SU[1], "M2": SU[2]}

    vec = P.sb("vecB", [128, 160], F32)
    P.dma("sp", vec[:, :], vecB[:, :], reads=[vecB.r()], writes=[vec])
    wo = P.sb("wo", [128, 8 * 1024], BF16)
    for k in range(8):
        P.dma("pool", wo[:, k * 1024:(k + 1) * 1024], w_out[k * 128:(k + 1) * 128, :], reads=[w_out.r()], writes=[wo], semkey="wo")
    pw = P.sb("pw", [64, 256], BF16)
    P.dma("pool", pw[:, :], pool_w[:, :], reads=[pool_w.r()], writes=[pw])
    ds = P.sb("dstrip", [128, 2944], BF16)
    P.dma("sp", ds[:, :], dstrip[:, :], reads=[dstrip.r()], writes=[ds])
    pst = P.sb("pstrip", [128, 4 * 1152], BF16)
    P.dma("sp", pst[:, :], pstrip[:, :], reads=[pstrip.r()], writes=[pst])
    ut = P.sb("utok", [128, 34 * 256], BF16)
    P.dma("sp", ut[:, :], uTok[:, :], reads=[uTok.r()], writes=[ut])
    ones64 = P.sb("ones64", [64, 64], F32)
    P.op("pool", lambda e: e.memset(ones64[:, :], 1.0 / 64.0), [], [ones64])
    onesLN = P.sb("onesLN", [128, 128], F32)
    P.op("pool", lambda e: e.memset(onesLN[:, :], 1.0 / 1024.0), [], [onesLN])
    lt = P.sb("lam_t", [128, 64], F32)
    ls = P.sb("lam_s", [128, 4], F32)
    P.op("dve", lambda e: e.tensor_tensor(out=lt[:, 0:32], in0=vec[:, 32:64], in1=vec[:, 64:96], op=ALU.mult), [vec], [lt])
    P.op("dve", lambda e: e.tensor_tensor(out=lt[:, 32:64], in0=vec[:, 96:128], in1=vec[:, 128:160], op=ALU.mult), [vec, lt], [lt])
    P.op("dve", lambda e: e.reduce_sum(out=ls[:, 0:1], in_=lt[:, 0:32], axis=mybir.AxisListType.X), [lt], [ls])
    P.op("dve", lambda e: e.reduce_sum(out=ls[:, 1:2], in_=lt[:, 32:64], axis=mybir.AxisListType.X), [lt, ls], [ls])
    P.op("act", lambda e: e.activation(out=ls[:, 0:2], in_=ls[:, 0:2], func=AF.Exp), [ls], [ls])
    P.op("dve", lambda e: e.tensor_tensor(out=ls[:, 2:3], in0=ls[:, 1:2], in1=ls[:, 0:1], op=ALU.subtract), [ls], [ls])
    P.op("dve", lambda e: e.tensor_tensor(out=ls[:, 2:3], in0=ls[:, 2:3], in1=vec[:, 21:22], op=ALU.subtract), [ls, vec], [ls])
    P.op("dve", lambda e: e.tensor_tensor(out=ls[:, 3:4], in0=vec[:, 20:21], in1=vec[:, 22:23], op=ALU.mult), [ls, vec], [ls])
    lam = ls

    epsr = make_eps(P, "epsr", RMS_EPS)
    KCH = 2048
    ktc = [P.sb("kt%d" % i, [128, KCH], BF16) for i in range(SEQ // KCH)]
    vxc = [P.sb("vx%d" % i, [128, KCH], BF16) for i in range(SEQ // KCH)]

    def load_kv(ksrc, krows, vsrc, nkeys):
        for ci in range(nkeys // KCH):
            P.dma("pool", ktc[ci][0:krows, :], ksrc(ci * KCH, (ci + 1) * KCH), reads=[QT.r()], writes=[ktc[ci]])
            P.dma("pool", vxc[ci][:, :], vsrc(ci * KCH, (ci + 1) * KCH), reads=[QT.r()], writes=[vxc[ci]])
    qb = [P.sb("q%d" % i, [128, T], BF16) for i in range(2)]
    pT = [P.sb("pT%d" % i, [128, 2 * T], BF16) for i in range(4)]
    pT2 = [P.sb("pTm%d" % i, [128, 2 * T], BF16) for i in range(4)]
    rden = [P.sb("rden%d" % i, [64, T], F32) for i in range(2)]
    o32 = [P.sb("o32_%d" % i, [64, T], F32) for i in range(2)]
    osq = P.sb("osq", [64, T], F32)
    orst = P.sb("orst", [64, T], F32)
    ob16 = [P.sb("ob16_%d" % i, [64, T], BF16) for i in range(2)]
    cnt = {"s": 0, "p": 0, "pm": 0, "q": 0, "ob": 0}

    LOOK = 2

    def attn(d, pbase, nblk, blk0, scale, acc, q, strip_delta0=None):
        pend = []
        npair = nblk // 2
        for i in range(npair + LOOK):
            if i < npair:
                su = SU[cnt["s"] % 3]
                cnt["s"] += 1
                p = pT[cnt["p"] % 4]
                cnt["p"] += 1
                for hh in range(2):
                    kb = blk0 + 2 * i + hh
                    kt = ktc[kb // 16]
                    ko = (kb % 16) * 128
                    P.op("pe", lambda e, kt=kt, ko=ko, su=su, hh=hh: e.matmul(su[:, hh * T:(hh + 1) * T], lhsT=kt[pbase:pbase + d, ko:ko + 128],
                                                                              rhs=q[pbase:pbase + d, :], start=True, stop=True), [kt, q], [su])
                P.op("act", lambda e, su=su, p=p: e.activation(out=p[:, :], in_=su[:, :], func=AF.Exp, scale=scale), [su], [p])
                if strip_delta0 is not None:
                    p2 = pT2[cnt["pm"] % 4]
                    cnt["pm"] += 1
                    for hh in range(2):
                        c0 = 1408 - 128 * (strip_delta0 + 2 * i + hh)
                        P.op("dve", lambda e, p=p, p2=p2, c0=c0, hh=hh: e.tensor_tensor(out=p2[:, hh * T:(hh + 1) * T], in0=p[:, hh * T:(hh + 1) * T],
                                                                                       in1=ds[:, c0:c0 + T], op=ALU.mult), [p, ds, p2], [p2])
                    p = p2
                pend.append((p, i))
            if i >= LOOK:
                p, j = pend.pop(0)
                for hh in range(2):
                    kb = blk0 + 2 * j + hh
                    vx = vxc[kb // 16]
                    vo = (kb % 16) * 128
                    P.op("pe", lambda e, vx=vx, vo=vo, p=p, j=j, hh=hh: e.matmul(acc[:, :], lhsT=vx[:, vo:vo + 128], rhs=p[:, hh * T:(hh + 1) * T],
                                                                                 start=(j == 0 and hh == 0), stop=(j == npair - 1 and hh == 1)),
                         [vx, p], [acc])

    def normalize(acc, out32, rd):
        P.op("dve", lambda e: e.reciprocal(out=rd[0:64, :], in_=acc[64:128, :]), [acc], [rd])
        P.op("dve", lambda e: e.tensor_tensor(out=out32[0:64, :], in0=acc[0:64, :], in1=rd[0:64, :], op=ALU.mult), [acc, rd], [out32])

    def store_mix(ob, row0, t):
        P.dma("sp", mixT[row0:row0 + 64, t * T:(t + 1) * T], ob[0:64, :], reads=[ob], writes=[mixT.r((row0 // 128, t))], semkey=ob.name)

    def load_q(row0, d, pbase, t):
        q = qb[cnt["q"] % 2]
        cnt["q"] += 1
        P.dma("sp", q[pbase:pbase + d, :], QT[row0:row0 + d, t * T:(t + 1) * T], reads=[QT.r()], writes=[q])
        return q

    for h in range(4):
        load_kv(lambda a, b, h=h: dKT[64 * h:64 * h + 64, a:b], 64, lambda a, b, h=h: dVx[h, :, a:b], SEQ)
        for t in range(NT):
            q = load_q(R_DQ + 64 * h, 64, 0, t)
            attn(32, 0, 128, 0, 32 ** -0.5, PS["ACC0"], q)
            attn(32, 32, 128, 0, 32 ** -0.5, PS["ACC1"], q)
            normalize(PS["ACC0"], o32[0], rden[0])
            normalize(PS["ACC1"], o32[1], rden[1])
            o = o32[0]
            P.op("dve", lambda e, o=o: e.scalar_tensor_tensor(out=o[0:64, :], in0=o32[1][0:64, :], scalar=lam[0:64, 2:3], in1=o[0:64, :],
                                                              op0=ALU.mult, op1=ALU.add), [o32[1], o, lam], [o])
            P.op("act", lambda e, o=o: e.activation(out=osq[0:64, :], in_=o[0:64, :], func=AF.Square), [o], [osq])
            P.op("pe", lambda e: e.matmul(PS["M0"][0:64, 0:T], lhsT=ones64[:, :], rhs=osq[0:64, :], start=True, stop=True),
                 [ones64, osq], [PS["M0"]])
            emit_rstd(P, orst, PS["M0"], epsr, 64, T)
            P.op("dve", lambda e, o=o: e.tensor_tensor(out=o[0:64, :], in0=o[0:64, :], in1=orst[0:64, :], op=ALU.mult), [o, orst], [o])
            ob = ob16[cnt["ob"] % 2]
            cnt["ob"] += 1
            P.op("dve", lambda e, o=o, ob=ob: e.tensor_scalar(out=ob[0:64, :], in0=o[0:64, :], scalar1=lam[0:64, 3:4], scalar2=None,
                                                              op0=ALU.mult), [o, lam], [ob])
            store_mix(ob, 64 * h, t)
    for h in range(4):
        load_kv(lambda a, b, h=h: mKT[h, :, a:b], 96, lambda a, b, h=h: mVx[h, :, a:b], SEQ)
        for t in range(NT):
            q = load_q(512 + 96 * h, 96, 0, t)
            acc = PS["ACC0"] if t % 2 == 0 else PS["ACC1"]
            attn(96, 0, 128, 0, 96 ** -0.5, acc, q)
            i = t % 2
            normalize(acc, o32[i], rden[i])
            ob = ob16[cnt["ob"] % 2]
            cnt["ob"] += 1
            P.op("act", lambda e, i=i, ob=ob: e.activation(out=ob[0:64, :], in_=o32[i][0:64, :], func=AF.Copy), [o32[i]], [ob])
            store_mix(ob, 768 + 64 * h, t)
    for h in range(4):
        load_kv(lambda a, b, h=h: bKT[64 * h:64 * h + 64, a:b], 64, lambda a, b, h=h: bVx[h, :, a:b], 6144)
        for t in range(NT):
            q = load_q(256 + 64 * h, 64, 0, t)
            acc = PS["ACC0"] if t % 2 == 0 else PS["ACC1"]
            attn(64, 0, 20, 4 * t, 64 ** -0.5, acc, q, strip_delta0=-8)
            i = t % 2
            normalize(acc, o32[i], rden[i])
            ob = ob16[cnt["ob"] % 2]
            cnt["ob"] += 1
            P.op("act", lambda e, i=i, ob=ob: e.activation(out=ob[0:64, :], in_=o32[i][0:64, :], func=AF.Copy), [o32[i]], [ob])
            store_mix(ob, 256 + 64 * h, t)
    ic = [P.sb("ic%d" % i, [64, T], F32) for i in range(2)]
    uu = [P.sb("uu%d" % i, [128, T], BF16) for i in range(2)]
    dd = [P.sb("dd%d" % i, [64, T], F32) for i in range(2)]
    db = [P.sb("db%d" % i, [64, T], BF16) for i in range(2)]
    n = 0
    for t in range(NT):
        for g in range(4):
            i = n % 2
            n += 1
            pp, py = PS["M1"], PS["M2"]
            for dl in range(-1, 5):
                blk = 4 * t + dl + 1
                c0 = g * 1152 + 512 - 128 * dl
                P.op("pe", lambda e, blk=blk, c0=c0, dl=dl, g=g: e.matmul(
                    pp[0:64, 0:T], lhsT=ut[:, blk * 256 + 64 * g:blk * 256 + 64 * g + 64], rhs=pst[:, c0:c0 + T],
                    start=(dl == -1), stop=(dl == 4)), [ut, pst], [pp])
            P.dma("sp", ic[i][:, :], invc[g, :, t * T:(t + 1) * T], reads=[invc.r()], writes=[ic[i]])
            ub = uu[i]
            P.dma("sp", ub[0:64, :], u16T[64 * g:64 * g + 64, t * T:(t + 1) * T], reads=[u16T.r()], writes=[ub])
            pb0 = 0
            P.op("dve", lambda e, i=i: e.tensor_tensor(out=dd[i][0:64, :], in0=pp[0:64, 0:T], in1=ic[i][0:64, :], op=ALU.mult),
                 [pp, ic[i]], [dd[i]])
            P.op("dve", lambda e, i=i, ub=ub, pb0=pb0: e.tensor_tensor(out=db[i][0:64, :], in0=dd[i][0:64, :], in1=ub[pb0:pb0 + 64, :],
                                                                       op=ALU.subtract), [dd[i], ub], [db[i]])
            P.op("pe", lambda e, i=i, g=g: e.matmul(py[0:64, 0:T], lhsT=pw[0:64, 64 * g:64 * g + 64], rhs=db[i][0:64, :],
                                                    start=True, stop=True), [pw, db[i]], [py])
            ob = ob16[cnt["ob"] % 2]
            cnt["ob"] += 1
            P.op("dve", lambda e, ob=ob, g=g: e.tensor_scalar(out=ob[0:64, :], in0=py[0:64, 0:T], scalar1=vec[0:64, 16 + g:17 + g],
                                                              scalar2=None, op0=ALU.mult), [py, vec], [ob])
            store_mix(ob, 512 + 64 * g, t)

    mx = [P.sb("mx%d" % k, [128, T], BF16) for k in range(8)]
    z = [P.sb("z%d" % k, [128, T], F32) for k in range(8)]
    xc = [P.sb("xc%d" % k, [128, T], F32) for k in range(2)]
    oc = [P.sb("oc%d" % k, [128, T], F32) for k in range(2)]
    tmp = {"mean": P.sb("ln_mean", [128, T], F32), "rstd": P.sb("ln_rstd", [128, T], F32),
           "sq": [P.sb("ln_sq%d" % k, [128, T], F32) for k in range(2)], "eps": make_eps(P, "epsln", LN_EPS)}
    for t in range(NT):
        for k in range(8):
            P.dma("sp", mx[k][:, :], mixT[k * 128:(k + 1) * 128, t * T:(t + 1) * T], reads=[mixT.r((k, t))], writes=[mx[k]])
        for o in range(8):
            yps = SU[o % 3]
            for k in range(8):
                P.op("pe", lambda e, k=k, o=o, yps=yps: e.matmul(yps[:, 0:T], lhsT=wo[:, k * 1024 + o * 128:k * 1024 + (o + 1) * 128],
                                                                 rhs=mx[k][:, :], start=(k == 0), stop=(k == 7)), [wo, mx[k]], [yps])
            x_ = xc[o % 2]
            P.dma("sp", x_[:, :], xT[o * 128:(o + 1) * 128, t * T:(t + 1) * T], reads=[xT.r()], writes=[x_])
            P.op("dve", lambda e, o=o, x_=x_, yps=yps: e.scalar_tensor_tensor(out=z[o][:, :], in0=x_[:, :], scalar=ALPHA, in1=yps[:, 0:T],
                                                                              op0=ALU.mult, op1=ALU.add), [x_, yps], [z[o]])
        outs = [oc[o % 2] for o in range(8)]

        def store(o, t=t):
            P.dma("sp", x1T[o * 128:(o + 1) * 128, t * T:(t + 1) * T], oc[o % 2][:, :], reads=[oc[o % 2]], writes=[x1T.r((o, t))],
                  semkey=oc[o % 2].name)
        emit_layernorm(P, PS, z, onesLN, vec, 0, 8, tmp, outs, store=store)
    P.finish("sp", x1T.allres())
    P.emit()
    return nc


TF = 256
def build_C2A():
    nc = bass.Bass("TRN2", target_bir_lowering=False)
    P = Prog(nc)
    d = declare_phaseA_inputs(P)
    x1h = DR(P, "x1h", [1024, TOK + 2], F32, "ExternalInput")
    w_up = DR(P, "w_up", [1024, 2 * D_FF], F32, "ExternalInput")
    w_dn = DR(P, "w_dn", [D_FF, 1024], F32, "ExternalInput")
    vecC = DR(P, "vecC", [128, 192], F32, "ExternalInput")
    x2T = DR(P, "x2T", [1024, TOK], F32, "ExternalOutput")
    OA = DR(P, "OA", [NR_OA, TOK], BF16, "ExternalOutput")
    PS = alloc_psum(P)
    wbig = P.sb("wbig", [128, 8 * 2 * D_FF], BF16)
    d["wbig"] = wbig
    for k in range(8):
        P.dma("pool", wbig[:, k * 5632:(k + 1) * 5632], w_up[k * 128:(k + 1) * 128, :], reads=[w_up.r()], writes=[wbig], semkey="wbig")
    wdn = P.sb("wdn", [128, 22 * 1024], BF16)
    for j in range(22):
        P.dma("pool", wdn[:, j * 1024:(j + 1) * 1024], w_dn[j * 128:(j + 1) * 128, :], reads=[w_dn.r()], writes=[wdn], semkey="wdn")
    vec = P.sb("vecC", [128, 192], F32)
    P.dma("sp", vec[:, :], vecC[:, :], reads=[vecC.r()], writes=[vec])
    onesLN = P.sb("onesLN", [128, 128], F32)
    P.op("pool", lambda e: e.memset(onesLN[:, :], 1.0 / 1024.0), [], [onesLN])

    x1c = [P.sb("x1c%d" % k, [128, TF + 2], F32) for k in range(8)]
    x1b = [P.sb("x1b%d" % k, [128, TF + 2], BF16) for k in range(8)]
    gb = [P.sb("gb%d" % j, [128, TF], BF16) for j in range(22)]
    acc = [P.sb("acc%d" % i, [128, TF], F32) for i in range(4)]
    sg = [P.sb("sg%d" % i, [128, TF], F32) for i in range(2)]
    z = [P.sb("z%d" % k, [128, TF], F32) for k in range(8)]
    oc = [P.sb("oc%d" % k, [128, TF], F32) for k in range(2)]
    tmp = {"mean": P.sb("ln_mean", [128, TF], F32), "rstd": P.sb("ln_rstd", [128, TF], F32),
           "sq": [P.sb("ln_sq%d" % k, [128, TF], F32) for k in range(2)], "eps": make_eps(P, "epsln", LN_EPS)}
    banks = [PS["S0"], PS["S1"], PS["S2"], PS["ACC0"], PS["ACC1"], PS["M2"]]
    nbk = 0
    na = 0
    for t in range(TOK // TF):
        c0 = t * TF
        for k in range(8):
            P.dma("sp", x1c[k][:, :], x1h[k * 128:(k + 1) * 128, c0:c0 + TF + 2], reads=[x1h.r()], writes=[x1c[k]])
            P.op("pool" if k % 2 == 0 else "dve", lambda e, k=k: e.tensor_copy(out=x1b[k][:, :], in_=x1c[k][:, :]), [x1c[k]], [x1b[k]])
        for j in range(22):
            accs = []
            for which in range(2):
                ch = which * 22 + j
                col0 = ch * 128
                ps = banks[nbk % 6]
                nbk += 1
                for k in range(8):
                    P.op("pe", lambda e, k=k, ps=ps, col0=col0: e.matmul(ps[:, 0:TF + 2], lhsT=wbig[:, k * 5632 + col0:k * 5632 + col0 + 128],
                                                                         rhs=x1b[k][:, 0:TF + 2], start=(k == 0), stop=(k == 7)),
                         [wbig, x1b[k]], [ps])
                a = acc[na % 4]
                na += 1
                w0, w1, w2, bb = (vec[:, 16 + ch:17 + ch], vec[:, 60 + ch:61 + ch], vec[:, 104 + ch:105 + ch], vec[:, 148 + ch:149 + ch])
                P.op("act", lambda e, a=a, ps=ps, w1=w1, bb=bb: e.activation(out=a[:, :], in_=ps[:, 1:TF + 1], func=AF.Identity, scale=w1, bias=bb),
                     [ps, vec], [a])
                P.op("dve", lambda e, a=a, ps=ps, w0=w0: e.scalar_tensor_tensor(out=a[:, :], in0=ps[:, 0:TF], scalar=w0, in1=a[:, :],
                                                                                op0=ALU.mult, op1=ALU.add), [ps, a, vec], [a])
                P.op("dve", lambda e, a=a, ps=ps, w2=w2: e.scalar_tensor_tensor(out=a[:, :], in0=ps[:, 2:TF + 2], scalar=w2, in1=a[:, :],
                                                                                op0=ALU.mult, op1=ALU.add), [ps, a, vec], [a])
                accs.append(a)
            s = sg[j % 2]
            P.op("act", lambda e, s=s, a=accs[0]: e.activation(out=s[:, :], in_=a[:, :], func=AF.Silu), [accs[0]], [s])
            P.op("pool", lambda e, s=s, a=accs[1], j=j: e.tensor_tensor(out=gb[j][:, :], in0=s[:, :], in1=a[:, :], op=ALU.mult),
                 [s, accs[1]], [gb[j]])
        for o in range(8):
            ps = banks[nbk % 6]
            nbk += 1
            for j in range(22):
                P.op("pe", lambda e, j=j, o=o, ps=ps: e.matmul(ps[:, 0:TF], lhsT=wdn[:, j * 1024 + o * 128:j * 1024 + (o + 1) * 128],
                                                               rhs=gb[j][:, :], start=(j == 0), stop=(j == 21)), [wdn, gb[j]], [ps])
            P.op("dve", lambda e, o=o, ps=ps: e.scalar_tensor_tensor(out=z[o][:, :], in0=x1c[o][:, 1:TF + 1], scalar=ALPHA, in1=ps[:, 0:TF],
                                                                     op0=ALU.mult, op1=ALU.add), [x1c[o], ps], [z[o]])
        outs = [oc[o % 2] for o in range(8)]

        def store(o, c0=c0, t=t):
            P.dma("sp", x2T[o * 128:(o + 1) * 128, c0:c0 + TF], oc[o % 2][:, :], reads=[oc[o % 2]], writes=[x2T.r((o, t // 2))],
                  semkey=oc[o % 2].name)
        emit_layernorm(P, PS, z, onesLN, vec, 0, 8, tmp, outs, ncols=TF, store=store)

    P.barrier()
    ar = Arena([wdn.t[:, :], wbig.t[:, 8 * NC_EXT:8 * 2 * D_FF]])
    W = emit_phaseA_setup(P, d, alloc=ar.alloc)
    xf = [ar.alloc("xf%d" % i, [128, T], F32) for i in range(2)]
    xb = [[ar.alloc("xb%d_%d" % (s, k), [128, T], BF16) for k in range(8)] for s in range(1)] * 2
    for t in range(NT):
        for k in range(8):
            f = xf[k % 2]
            P.dma("sp", f[:, :], x2T[k * 128:(k + 1) * 128, t * T:(t + 1) * T], reads=[x2T.r((k, t))], writes=[f])
            P.op("dve" if k % 2 == 0 else "pool", lambda e, f=f, k=k, t=t: e.tensor_copy(out=xb[t % 2][k][:, :], in_=f[:, :]),
                 [f], [xb[t % 2][k]])
        emit_phaseA_tile(P, PS, W, d, t, xb[t % 2], OA)
    P.finish("sp", OA.allres() + x2T.allres())
    P.emit()
    return nc


_PROGS = {}


def _prog(name):
    if name not in _PROGS:
        _PROGS[name] = {"A": build_A, "BC1": build_BC1, "C2A": build_C2A}[name]()
    return _PROGS[name]


def _run(name, in_maps):
    res = run_bass_kernel_spmd(_prog(name), in_maps, core_ids=list(range(NCORE)))
    return res.results


def _rope_perm(ncols, group, rot):
    half = rot // 2
    idx = np.arange(ncols)
    j = idx % group
    out = idx.copy()
    out[j < half] += half
    out[(j >= half) & (j < rot)] -= half
    return out


def _w_ext(w_in_l):
    a, b, c = 768, 1536, 1792
    dq, dk, dv = w_in_l[:, 0:256], w_in_l[:, 256:512], w_in_l[:, 512:768]
    bq, bk, bv = w_in_l[:, 768:1024], w_in_l[:, 1024:1280], w_in_l[:, 1280:1536]
    u = w_in_l[:, 1536:1792]
    cq, ckv, kr = w_in_l[:, 1792:2048], w_in_l[:, 2048:2176], w_in_l[:, 2176:2208]
    pd = _rope_perm(256, 32, 8)
    pb = _rope_perm(256, 64, 16)
    pk = _rope_perm(32, 32, 32)
    pad = np.zeros((1024, 64), np.float32)
    return np.ascontiguousarray(np.concatenate(
        [dq, dk, dq[:, pd], dk[:, pd], dv, bq, bk, bq[:, pb], bk[:, pb], bv, u, cq, ckv, kr, kr[:, pk], pad], axis=1))


def _wuq_ext(w_uq_l):
    cols = []
    pm = _rope_perm(32, 32, 32)
    for h in range(4):
        blk = w_uq_l[:, 96 * h:96 * h + 96]
        sw = np.concatenate([blk[:, 0:64], blk[:, 64:96][:, pm]], axis=1)
        cols += [blk, sw]
    return np.ascontiguousarray(np.concatenate(cols, axis=1))


def _rope_rows(n, group, rot, offset=0):
    half = rot // 2
    f = np.zeros(128, np.float32)
    s = np.zeros(128, np.float32)
    inv = (THETA ** (-(np.arange(half, dtype=np.float32) * 2.0 / rot))).astype(np.float32)
    for r in range(n):
        j = r % group - offset
        if 0 <= j < half:
            f[r], s[r] = inv[j], -1.0
        elif half <= j < rot:
            f[r], s[r] = inv[j - half], 1.0
    return f, s


def _vecA(q_norm_l, kv_norm_l):
    v = np.zeros((128, 16), np.float32)
    for i, (n, group, rot, off) in enumerate(((128, 32, 8, 0), (128, 64, 16, 0), (96, 96, 32, 64), (32, 32, 32, 0))):
        v[:, 2 * i], v[:, 2 * i + 1] = _rope_rows(n, group, rot, off)
    v[:, 8] = q_norm_l[0:128]
    v[:, 9] = q_norm_l[128:256]
    v[:, 10] = kv_norm_l
    return v


def _phaseA_inputs(inp, l, c):
    b, i = divmod(c, 4)
    return {"w_ext": _w_ext(inp["w_in"][l]), "wuq_ext": _wuq_ext(inp["mla_w_uq"][l]),
            "w_ukv": np.ascontiguousarray(inp["mla_w_ukv"][l]), "vecA": _vecA(inp["mla_q_norm"][l], inp["mla_kv_norm"][l]),
            "pos": np.ascontiguousarray(inp["positions"][b, i * TOK:(i + 1) * TOK].reshape(1, TOK))}


def _vext(vT):
    n = vT.shape[1]
    e = np.ones((n, 128), NPBF)
    e[:, 0:64] = vT.T
    return e


def _blocked(e):
    n = e.shape[0]
    return np.ascontiguousarray(e.reshape(n // 128, 128, e.shape[1]).transpose(1, 0, 2).reshape(128, -1))


_CONST = {}


def _consts():
    if _CONST:
        return _CONST
    dd = np.arange(-1408, 2944 - 1408 + 0)
    p = np.arange(128)[:, None]
    c = np.arange(2944)[None, :]
    dlt = p - c + 1408
    ad = np.abs(dlt)
    m = (ad <= 64).astype(np.float32) + ((dlt % 4 == 0) & (ad <= 256)) + ((dlt % 16 == 0) & (ad <= 1024))
    _CONST["dstrip"] = m.astype(NPBF)
    c = np.arange(1152)[None, :]
    dlt = p - c + 512
    ps = []
    for w in (2, 4, 8, 16):
        ps.append(((dlt >= -(w // 2)) & (dlt <= w // 2 - 1)).astype(np.float32))
    _CONST["pstrip"] = np.ascontiguousarray(np.concatenate(ps, axis=1).astype(NPBF))
    tt = np.arange(SEQ)
    ic = np.zeros((4, SEQ), np.float32)
    for g, w in enumerate((2, 4, 8, 16)):
        lo = np.clip(tt - w // 2, 0, SEQ - 1)
        hi = np.clip(tt + w - w // 2 - 1, 0, SEQ - 1)
        ic[g] = 1.0 / (hi - lo + 1).astype(np.float32)
    _CONST["invc"] = ic
    return _CONST


def _bc1_inputs(inp, l, OAs, xTs):
    cst = _consts()
    maps = []
    lam_init = 0.8 - 0.6 * math.exp(-0.3 * l)
    vecB = np.zeros((128, 160), np.float32)
    vecB[:, 0:8] = inp["ln1_g"][l].reshape(8, 128).T
    vecB[:, 8:16] = inp["ln1_b"][l].reshape(8, 128).T
    vecB[0:64, 16:20] = inp["pool_scale"][l].reshape(4, 64).T
    vecB[0:64, 20] = inp["diff_subln"][l]
    vecB[:, 21] = lam_init
    vecB[:, 22] = 1.0 - lam_init
    vecB[:, 32:160] = inp["diff_lambda"][l].reshape(1, 128)
    pool_w = np.ascontiguousarray(inp["pool_w"][l].transpose(1, 0, 2).reshape(64, 256))
    w_out = np.ascontiguousarray(inp["w_out"][l])
    for b in range(2):
        OAb = np.concatenate([OAs[4 * b + i] for i in range(4)], axis=1)
        dKT = np.ascontiguousarray(OAb[R_DK:R_DK + 256])
        dVx = np.stack([_blocked(_vext(OAb[R_DV + 64 * h:R_DV + 64 * h + 64])) for h in range(4)])
        mKT = np.stack([np.concatenate([OAb[R_MKV + 128 * h:R_MKV + 128 * h + 64], OAb[R_MKR:R_MKR + 32]], axis=0) for h in range(4)])
        mVx = np.stack([_blocked(_vext(OAb[R_MKV + 128 * h + 64:R_MKV + 128 * h + 128])) for h in range(4)])
        bKp = np.zeros((256, SEQ + 2048), NPBF)
        bKp[:, 1024:1024 + SEQ] = OAb[R_BK:R_BK + 256]
        bVp = []
        for h in range(4):
            e = np.zeros((SEQ + 2048, 128), NPBF)
            e[1024:1024 + SEQ] = _vext(OAb[R_BV + 64 * h:R_BV + 64 * h + 64])
            bVp.append(e)
        utp = np.zeros((SEQ + 256, 256), NPBF)
        utp[128:128 + SEQ] = OAb[R_U:R_U + 256].T
        for i in range(4):
            c = 4 * b + i
            OAc = OAs[c]
            m = {
                "QT": np.ascontiguousarray(np.concatenate([OAc[R_DQ:R_DQ + 256], OAc[R_BQ:R_BQ + 256], OAc[R_MQ:R_MQ + 384]], axis=0)),
                "dKT": dKT, "dVx": dVx, "mKT": np.ascontiguousarray(mKT), "mVx": mVx,
                "bKT": np.ascontiguousarray(bKp[:, TOK * i:TOK * i + 6144]),
                "bVx": np.stack([_blocked(bVp[h][TOK * i:TOK * i + 6144]) for h in range(4)]),
                "uTok": _blocked(utp[TOK * i:TOK * i + 34 * 128]),
                "u16T": np.ascontiguousarray(OAc[R_U:R_U + 256]),
                "invc": np.ascontiguousarray(np.broadcast_to(cst["invc"][:, None, TOK * i:TOK * (i + 1)], (4, 64, TOK))),
                "dstrip": cst["dstrip"], "pstrip": cst["pstrip"],
                "xT": xTs[c], "w_out": w_out, "pool_w": pool_w, "vecB": vecB,
            }
            maps.append(m)
    return maps


def _c2a_inputs(inp, l, lnext, x1Ts):
    vecC = np.zeros((128, 192), np.float32)
    vecC[:, 0:8] = inp["ln2_g"][l].reshape(8, 128).T
    vecC[:, 8:16] = inp["ln2_b"][l].reshape(8, 128).T
    for j in range(3):
        vecC[:, 16 + 44 * j:16 + 44 * (j + 1)] = inp["ffn_conv_w"][l][j].reshape(44, 128).T
    vecC[:, 148:192] = inp["ffn_conv_b"][l].reshape(44, 128).T
    w_up = np.ascontiguousarray(inp["ffn_w_up"][l])
    w_dn = np.ascontiguousarray(inp["ffn_w_down"][l])
    maps = []
    for b in range(2):
        xb = np.zeros((1024, SEQ + 2), np.float32)
        xb[:, 1:1 + SEQ] = np.concatenate([x1Ts[4 * b + i] for i in range(4)], axis=1)
        for i in range(4):
            c = 4 * b + i
            m = _phaseA_inputs(inp, lnext, c)
            m.update({"x1h": np.ascontiguousarray(xb[:, TOK * i:TOK * i + TOK + 2]), "w_up": w_up, "w_dn": w_dn, "vecC": vecC})
            maps.append(m)
    return maps


def kernel(**inp):
    inp = {k: np.asarray(v) for k, v in inp.items()}
    x = inp["x"]
    xTs = [np.ascontiguousarray(x[c // 4, (c % 4) * TOK:(c % 4 + 1) * TOK, :].T) for c in range(NCORE)]
    maps = []
    for c in range(NCORE):
        m = _phaseA_inputs(inp, 0, c)
        m["xT"] = xTs[c]
        maps.append(m)
    r = _run("A", maps)
    OAs = [np.asarray(r[c]["OA"]) for c in range(NCORE)]
    for l in range(DEPTH):
        r = _run("BC1", _bc1_inputs(inp, l, OAs, xTs))
        x1Ts = [np.asarray(r[c]["x1T"]) for c in range(NCORE)]
        r = _run("C2A", _c2a_inputs(inp, l, (l + 1) % DEPTH, x1Ts))
        xTs = [np.asarray(r[c]["x2T"]) for c in range(NCORE)]
        OAs = [np.asarray(r[c]["OA"]) for c in range(NCORE)]
    out = np.zeros((2, SEQ, D_MODEL), np.float32)
    for c in range(NCORE):
        out[c // 4, (c % 4) * TOK:(c % 4 + 1) * TOK, :] = xTs[c].T
    return out
```
